# Optimizing a Trainium2 kernel written in Bass

```python
import jax
import jax.numpy as jnp
from jax import lax
import numpy as np

D_MODEL = 1024
BATCH = 8
SEQ = 4096
DEPTH = 4

GRID_W = 64
CTX_LEN = 256
EPS = 1e-6
N_BRANCH = 4
BRANCH_W = D_MODEL // 2

CONV_W = BRANCH_W
CONV_K = 3
LRU_W = BRANCH_W
LRU_HEADS = 8
LRU_HD = LRU_W // LRU_HEADS
LRU_CONV_K = 4
LRU_C = 8.0
CMLP_W = BRANCH_W
CMLP_GROUPS = 4
CMLP_GD = CMLP_W // CMLP_GROUPS
CHUNK = 128
MLA_HEADS = 8
QK_NOPE = 64
QK_ROPE = 32
V_HD = BRANCH_W // MLA_HEADS
Q_LORA = 384
KV_LORA = 256
MLA_SCALE = (QK_NOPE + QK_ROPE) ** -0.5
ROPE_THETA = 10000.0
Q_BLOCK = 128
D_FF = -(-8 * D_MODEL // (3 * 256)) * 256

SPLIT_SIZES = (LRU_W, KV_LORA, QK_ROPE, LRU_W, Q_LORA, CONV_W, CONV_W, CONV_W, CMLP_W, CMLP_W, N_BRANCH * D_MODEL)
IN_W = sum(SPLIT_SIZES)
CTX_STATE_COLS = LRU_W + KV_LORA + QK_ROPE

kernel_name = 'hybrid_flow_backbone'


def rms_norm(x, g):
    x32 = x.astype(jnp.float32)
    y = x32 * lax.rsqrt(jnp.mean(x32 * x32, axis=-1, keepdims=True) + EPS)
    return (y * g.astype(jnp.float32)).astype(x.dtype)


def layer_norm(x, g, b):
    x32 = x.astype(jnp.float32)
    mu = jnp.mean(x32, axis=-1, keepdims=True)
    var = jnp.mean(jnp.square(x32 - mu), axis=-1, keepdims=True)
    y = (x32 - mu) * lax.rsqrt(var + EPS) * g.astype(jnp.float32) + b.astype(jnp.float32)
    return y.astype(x.dtype)


def modulate(h, shift, scale):
    return h * (1 + scale) + shift


def split_cols(z, sizes):
    idx = tuple(int(i) for i in np.cumsum(sizes)[:-1])
    return jnp.split(z, idx, axis=-1)


def dwconv(x, w, pad_left):
    k, ch = w.shape
    return lax.conv_general_dilated(
        x, w[:, None, :].astype(x.dtype), window_strides=(1,),
        padding=((pad_left, k - 1 - pad_left),),
        dimension_numbers=('NWC', 'WIO', 'NWC'), feature_group_count=ch)


def axial_rope_tables(row, col):
    n_freq = QK_ROPE // 4
    inv = ROPE_THETA ** (-jnp.arange(n_freq, dtype=jnp.float32) / n_freq)
    ang_r = row.astype(jnp.float32)[:, None] * inv
    ang_c = col.astype(jnp.float32)[:, None] * inv
    return jnp.cos(ang_r), jnp.sin(ang_r), jnp.cos(ang_c), jnp.sin(ang_c)


def _rotate(x, cos, sin):
    m = x.shape[-1] // 2
    x1, x2 = x[..., :m], x[..., m:]
    cos = cos.astype(x.dtype)
    sin = sin.astype(x.dtype)
    return jnp.concatenate([x1 * cos - x2 * sin, x1 * sin + x2 * cos], axis=-1)


def axial_rope(x, cos_r, sin_r, cos_c, sin_c):
    half = QK_ROPE // 2
    return jnp.concatenate([_rotate(x[..., :half], cos_r, sin_r), _rotate(x[..., half:], cos_c, sin_c)], axis=-1)


def short_conv_mixer(a_b, a_c, a_x, w_conv):
    return a_b * dwconv(a_c * a_x, w_conv, CONV_K // 2)


def rglru_coeffs(xc, w_a, b_a, w_x, b_x, lam):
    bn, n, _ = xc.shape
    x32 = xc.astype(jnp.float32)
    xh = x32.reshape(bn, n, LRU_HEADS, LRU_HD)
    r = jax.nn.sigmoid(jnp.einsum('blhi,hij->blhj', xh, w_a.astype(jnp.float32)).reshape(bn, n, LRU_W) + b_a.astype(jnp.float32))
    i = jax.nn.sigmoid(jnp.einsum('blhi,hij->blhj', xh, w_x.astype(jnp.float32)).reshape(bn, n, LRU_W) + b_x.astype(jnp.float32))
    log_a = -LRU_C * r * jax.nn.softplus(-lam.astype(jnp.float32))
    a = jnp.exp(log_a)
    b = jnp.sqrt(-jnp.expm1(2.0 * log_a)) * (i * x32)
    return a, b


def _scan_combine(left, right):
    a_l, b_l = left
    a_r, b_r = right
    return a_l * a_r, a_r * b_l + b_r


def linear_scan(a, b, h0, reverse):
    if h0 is not None:
        edge = -1 if reverse else 0
        b = b.at[:, edge].add(a[:, edge] * h0)
    return lax.associative_scan(_scan_combine, (a, b), reverse=reverse, axis=1)[1]


def rglru_bidirectional(x_lat, x_ctx, conv_w, conv_b, w_a, b_a, w_x, b_x, lam, need_ctx):
    xl = dwconv(x_lat, conv_w, LRU_CONV_K // 2) + conv_b
    xc = dwconv(x_ctx, conv_w, LRU_CONV_K // 2) + conv_b
    lat_sum = None
    ctx_sum = None
    for d, rev in enumerate((False, True)):
        a_c, b_c = rglru_coeffs(xc, w_a[d], b_a[d], w_x[d], b_x[d], lam[d])
        h_c = linear_scan(a_c, b_c, None, rev)
        h0 = h_c[:, 0] if rev else h_c[:, -1]
        a_l, b_l = rglru_coeffs(xl, w_a[d], b_a[d], w_x[d], b_x[d], lam[d])
        h_l = linear_scan(a_l, b_l, h0, rev)
        lat_sum = h_l if lat_sum is None else lat_sum + h_l
        if need_ctx:
            ctx_sum = h_c if ctx_sum is None else ctx_sum + h_c
    ctx_out = ctx_sum.astype(x_ctx.dtype) if need_ctx else None
    return lat_sum.astype(x_lat.dtype), ctx_out


def chunk_mlp_mixer(u_pre, v_pre, ln_g, ln_b, w_s, b_s):
    bn, n, _ = u_pre.shape
    u = jax.nn.gelu(u_pre)
    v = layer_norm(jax.nn.gelu(v_pre), ln_g, ln_b)
    vc = v.reshape(bn, n // CHUNK, CHUNK, CMLP_GROUPS, CMLP_GD)
    mixed = jnp.einsum('gpq,bnqgd->bnpgd', w_s, vc) + b_s.T[None, None, :, :, None]
    return u * mixed.reshape(bn, n, CMLP_W)


def mla_keys_values(kv_lat, k_rope, kv_norm_g, w_kv_up, rope):
    bn, n, _ = kv_lat.shape
    kv = (rms_norm(kv_lat, kv_norm_g) @ w_kv_up).reshape(bn, n, MLA_HEADS, QK_NOPE + V_HD)
    k_nope, v = kv[..., :QK_NOPE], kv[..., QK_NOPE:]
    if rope is not None:
        k_rope = axial_rope(k_rope, *rope)
    k_rope = jnp.broadcast_to(k_rope[:, :, None, :], (bn, n, MLA_HEADS, QK_ROPE))
    return jnp.concatenate([k_nope, k_rope], axis=-1), v


def mla_queries(q_lat, q_norm_g, w_q_up, rope):
    bn, n, _ = q_lat.shape
    q = (rms_norm(q_lat, q_norm_g) @ w_q_up).reshape(bn, n, MLA_HEADS, QK_NOPE + QK_ROPE)
    if rope is None:
        return q
    return jnp.concatenate([q[..., :QK_NOPE], axial_rope(q[..., QK_NOPE:], *rope)], axis=-1)


def softmax_attention(q, k, v):
    s = jnp.einsum('bqhd,bkhd->bhqk', q, k).astype(jnp.float32) * MLA_SCALE
    p = jax.nn.softmax(s, axis=-1).astype(v.dtype)
    return jnp.einsum('bhqk,bkhd->bqhd', p, v)


def latent_attention(q, k_all, v_all):
    bn, n, h, dk = q.shape
    qb = jnp.moveaxis(q.reshape(bn, n // Q_BLOCK, Q_BLOCK, h, dk), 1, 0)
    ob = lax.map(lambda qq: softmax_attention(qq, k_all, v_all), qb)
    return jnp.moveaxis(ob, 0, 1).reshape(bn, n, h * V_HD)


def merge_branches(ys, gate_pre, w_branch, w_out):
    d = w_out.shape[0]
    g = jax.nn.sigmoid(gate_pre)
    merged = g[..., :d] * (ys[0] @ w_branch[0])
    for n in range(1, N_BRANCH):
        merged = merged + g[..., n * d:(n + 1) * d] * (ys[n] @ w_branch[n])
    return merged @ w_out


def swiglu(h, w1, w3, w2):
    return (jax.nn.silu(h @ w1) * (h @ w3)) @ w2


def setup_inputs(seed: int = 0) -> dict:
    key = jax.random.key(seed)
    k = jax.random.split(key, 32)
    f32 = jnp.float32

    def nrm(i, shape, scale):
        return scale * jax.random.normal(k[i], shape, f32)

    L = DEPTH
    a_target = jax.random.uniform(k[16], (L, 2, LRU_W), f32, 0.9, 0.999)
    a_base = a_target ** (1.0 / LRU_C)
    lam = jnp.log(a_base) - jnp.log1p(-a_base)
    return {
        'x': nrm(0, (BATCH, SEQ, D_MODEL), 1.0),
        'c': nrm(1, (BATCH, D_MODEL), 1.0),
        'ctx': nrm(2, (BATCH, CTX_LEN, D_MODEL), 1.0),
        'c_ctx': nrm(3, (D_MODEL,), 1.0),
        'w_mod': nrm(4, (L, D_MODEL, 6 * D_MODEL), 0.5 * D_MODEL ** -0.5),
        'b_mod': nrm(5, (L, 6 * D_MODEL), 0.01),
        'norm1_g': 1.0 + nrm(6, (L, D_MODEL), 0.05),
        'norm2_g': 1.0 + nrm(7, (L, D_MODEL), 0.05),
        'w_in': nrm(8, (L, D_MODEL, IN_W), D_MODEL ** -0.5),
        'conv_a_w': nrm(9, (L, CONV_K, CONV_W), CONV_K ** -0.5),
        'lru_conv_w': nrm(10, (L, LRU_CONV_K, LRU_W), LRU_CONV_K ** -0.5),
        'lru_conv_b': nrm(11, (L, LRU_W), 0.01),
        'lru_w_a': nrm(12, (L, 2, LRU_HEADS, LRU_HD, LRU_HD), LRU_HD ** -0.5),
        'lru_b_a': nrm(13, (L, 2, LRU_W), 0.01),
        'lru_w_x': nrm(14, (L, 2, LRU_HEADS, LRU_HD, LRU_HD), LRU_HD ** -0.5),
        'lru_b_x': nrm(15, (L, 2, LRU_W), 0.01),
        'lru_lam': lam,
        'cmlp_ln_g': 1.0 + nrm(17, (L, CMLP_W), 0.05),
        'cmlp_ln_b': nrm(18, (L, CMLP_W), 0.01),
        'cmlp_w_s': nrm(19, (L, CMLP_GROUPS, CHUNK, CHUNK), CHUNK ** -0.5),
        'cmlp_b_s': 1.0 + nrm(20, (L, CMLP_GROUPS, CHUNK), 0.05),
        'mla_q_norm_g': 1.0 + nrm(21, (L, Q_LORA), 0.05),
        'mla_kv_norm_g': 1.0 + nrm(22, (L, KV_LORA), 0.05),
        'mla_w_q_up': nrm(23, (L, Q_LORA, MLA_HEADS * (QK_NOPE + QK_ROPE)), Q_LORA ** -0.5),
        'mla_w_kv_up': nrm(24, (L, KV_LORA, MLA_HEADS * (QK_NOPE + V_HD)), KV_LORA ** -0.5),
        'w_branch': nrm(25, (L, N_BRANCH, BRANCH_W, D_MODEL), BRANCH_W ** -0.5),
        'w_out': nrm(26, (L, D_MODEL, D_MODEL), D_MODEL ** -0.5),
        'w_ff1': nrm(27, (L, D_MODEL, D_FF), D_MODEL ** -0.5),
        'w_ff3': nrm(28, (L, D_MODEL, D_FF), D_MODEL ** -0.5),
        'w_ff2': nrm(29, (L, D_FF, D_MODEL), D_FF ** -0.5),
        'final_norm_g': 1.0 + nrm(30, (D_MODEL,), 0.05),
    }


def reference(x, c, ctx, c_ctx, w_mod, b_mod, norm1_g, norm2_g, w_in, conv_a_w,
              lru_conv_w, lru_conv_b, lru_w_a, lru_b_a, lru_w_x, lru_b_x, lru_lam,
              cmlp_ln_g, cmlp_ln_b, cmlp_w_s, cmlp_b_s, mla_q_norm_g, mla_kv_norm_g,
              mla_w_q_up, mla_w_kv_up, w_branch, w_out, w_ff1, w_ff3, w_ff2, final_norm_g):
    d = x.shape[-1]
    n_lat = x.shape[1]
    rows = n_lat // GRID_W
    row = jnp.repeat(jnp.arange(rows), GRID_W)
    col = jnp.tile(jnp.arange(GRID_W), rows)
    rope_k = axial_rope_tables(row, col)
    rope_q = tuple(t[:, None, :] for t in rope_k)
    s_lat = jax.nn.silu(c)
    s_ctx = jax.nn.silu(c_ctx)
    xc = ctx
    for l in range(DEPTH):
        last = l == DEPTH - 1
        sh1, sc1, g1, sh2, sc2, g2 = (m[:, None, :] for m in jnp.split(s_lat @ w_mod[l] + b_mod[l], 6, axis=-1))
        if last:
            sh1c, sc1c = jnp.split(s_ctx @ w_mod[l][:, :2 * d] + b_mod[l][:2 * d], 2, axis=-1)
        else:
            sh1c, sc1c, g1c, sh2c, sc2c, g2c = jnp.split(s_ctx @ w_mod[l] + b_mod[l], 6, axis=-1)

        h = modulate(rms_norm(x, norm1_g[l]), sh1, sc1)
        hc = modulate(rms_norm(xc, norm1_g[l]), sh1c, sc1c)
        (lru_x, kv_lat, k_rope, lru_g, q_lat, a_b, a_c, a_x, c_u, c_v, gate_pre) = split_cols(h @ w_in[l], SPLIT_SIZES)
        if last:
            lru_xc, kv_latc, k_ropec = split_cols(hc @ w_in[l][:, :CTX_STATE_COLS], SPLIT_SIZES[:3])
        else:
            (lru_xc, kv_latc, k_ropec, lru_gc, q_latc, a_bc, a_cc, a_xc, c_uc, c_vc, gate_prec) = split_cols(hc @ w_in[l], SPLIT_SIZES)

        h_lru, hc_lru = rglru_bidirectional(lru_x, lru_xc, lru_conv_w[l], lru_conv_b[l], lru_w_a[l], lru_b_a[l],
                                            lru_w_x[l], lru_b_x[l], lru_lam[l], not last)
        y_b = jax.nn.gelu(lru_g) * h_lru
        k_lat, v_lat = mla_keys_values(kv_lat, k_rope, mla_kv_norm_g[l], mla_w_kv_up[l], rope_k)
        k_ctx, v_ctx = mla_keys_values(kv_latc, k_ropec, mla_kv_norm_g[l], mla_w_kv_up[l], None)
        q = mla_queries(q_lat, mla_q_norm_g[l], mla_w_q_up[l], rope_q)
        y_d = latent_attention(q, jnp.concatenate([k_lat, k_ctx], axis=1), jnp.concatenate([v_lat, v_ctx], axis=1))
        y_a = short_conv_mixer(a_b, a_c, a_x, conv_a_w[l])
        y_c = chunk_mlp_mixer(c_u, c_v, cmlp_ln_g[l], cmlp_ln_b[l], cmlp_w_s[l], cmlp_b_s[l])

        x = x + g1 * merge_branches((y_a, y_b, y_c, y_d), gate_pre, w_branch[l], w_out[l])
        h2 = modulate(rms_norm(x, norm2_g[l]), sh2, sc2)
        x = x + g2 * swiglu(h2, w_ff1[l], w_ff3[l], w_ff2[l])

        if not last:
            yc_b = jax.nn.gelu(lru_gc) * hc_lru
            qc = mla_queries(q_latc, mla_q_norm_g[l], mla_w_q_up[l], None)
            oc = softmax_attention(qc, k_ctx, v_ctx)
            yc_d = oc.reshape(oc.shape[0], oc.shape[1], MLA_HEADS * V_HD)
            yc_a = short_conv_mixer(a_bc, a_cc, a_xc, conv_a_w[l])
            yc_c = chunk_mlp_mixer(c_uc, c_vc, cmlp_ln_g[l], cmlp_ln_b[l], cmlp_w_s[l], cmlp_b_s[l])
            xc = xc + g1c * merge_branches((yc_a, yc_b, yc_c, yc_d), gate_prec, w_branch[l], w_out[l])
            h2c = modulate(rms_norm(xc, norm2_g[l]), sh2c, sc2c)
            xc = xc + g2c * swiglu(h2c, w_ff1[l], w_ff3[l], w_ff2[l])
    return rms_norm(x, final_norm_g)
```

```python
import numpy as np
from contextlib import ExitStack
import concourse.bass as bass
import concourse.mybir as mybir
from concourse.bass_utils import run_bass_kernel_spmd

F32 = mybir.dt.float32
BF16 = mybir.dt.bfloat16
AF = mybir.ActivationFunctionType
ALU = mybir.AluOpType

D = 1024
S = 4096
CT = 256
T = S + CT
L = 4
INW = 8352
DFF = 2816
EPS = 1e-6
SCALE = 96.0 ** -0.5
TILES = [(0, 256)] + [(256 + 512 * i, 512) for i in range(8)]
NDMA_SEM = 24

C_LRUX, C_KV, C_KR, C_LRUG, C_QL, C_CB, C_CC, C_CX, C_CU, C_CV, C_GATE = (
    0, 512, 768, 800, 1312, 1696, 2208, 2720, 3232, 3744, 4256)


class Tile:
    __slots__ = ("name", "w", "readers", "psum")

    def __init__(self, name="", psum=False):
        self.name = name
        self.w = None
        self.readers = []
        self.psum = psum


class Prog:
    ENG = ("pe", "act", "dve", "pool", "sp")
    CE = ("pe", "act", "dve", "pool")

    def __init__(self, nc, stack):
        self.nc = nc
        self.sem = {e: stack.enter_context(nc.semaphore("s_" + e)) for e in self.CE}
        self.dsem = [stack.enter_context(nc.semaphore("d%d" % i)) for i in range(NDMA_SEM)]
        self.cnt = {e: 0 for e in self.CE}
        self.ndma = 0
        self.known = {e: {f: 0 for f in self.CE} for e in self.ENG}
        self.kdma = {e: set() for e in self.ENG}
        self.ops = {e: [] for e in self.ENG}

    def _deps(self, eng, reads, writes):
        deps = []
        for t in reads:
            if t.w is not None:
                deps.append(t.w)
            if t.psum:
                deps.extend(r for r in t.readers if r[0] != eng)
        for t in writes:
            if t.w is not None and t.w[0] != eng:
                deps.append(t.w)
            deps.extend(r for r in t.readers if r[0] != eng)
        best = {}
        kn = self.known[eng]
        for (e, idx) in deps:
            if e == "dma":
                if idx in self.kdma[eng]:
                    continue
                self.kdma[eng].add(idx)
                best[("d", idx)] = (self.dsem[idx % NDMA_SEM], 16 * (idx // NDMA_SEM + 1))
            else:
                if idx <= kn[e]:
                    continue
                kn[e] = idx
                best[e] = (self.sem[e], idx)
        return list(best.values())

    def _mark(self, me, reads, writes):
        for t in reads:
            t.readers.append(me)
        for t in writes:
            t.w = me
            t.readers = []

    def op(self, eng, writes, reads, fn):
        waits = self._deps(eng, reads, writes)
        if eng == "pe":
            waits = [(s, v) for (s, v) in waits if s is not self.sem["pe"]]
        self.cnt[eng] += 1
        self.ops[eng].append((waits, fn, (self.sem[eng], 1)))
        self._mark((eng, self.cnt[eng]), reads, writes)

    def dma(self, writes, reads, out_ap, in_ap, q="sp"):
        k = self.ndma
        self.ndma += 1
        waits = self._deps(q, reads, writes)
        if k >= NDMA_SEM:
            kk = k - NDMA_SEM
            if kk not in self.kdma[q]:
                self.kdma[q].add(kk)
                waits.append((self.dsem[kk % NDMA_SEM], 16 * (kk // NDMA_SEM + 1)))

        def fn(e, out_ap=out_ap, in_ap=in_ap):
            return e.dma_start(out=out_ap, in_=in_ap)
        self.ops[q].append((waits, fn, (self.dsem[k % NDMA_SEM], 16)))
        self._mark(("dma", k), reads, writes)

    def flush(self):
        nc = self.nc
        tail = []
        for k in range(max(0, self.ndma - NDMA_SEM), self.ndma):
            tail.append((self.dsem[k % NDMA_SEM], 16 * (k // NDMA_SEM + 1)))
        finals = [(self.sem[e], self.cnt[e]) for e in self.CE if self.cnt[e] > 0]
        ops = self.ops

        def replay(handle, lst, extra):
            for waits, fn, inc in lst:
                for s, v in waits:
                    handle.wait_ge(s, v)
                fn(handle).then_inc(inc[0], inc[1])
            for s, v in extra:
                handle.wait_ge(s, v)

        with nc.Block() as block:
            @block.tensor
            def _(e):
                replay(e, ops["pe"], finals)

            @block.scalar
            def _(e):
                replay(e, ops["act"], finals)

            @block.vector
            def _(e):
                replay(e, ops["dve"], finals)

            @block.gpsimd
            def _(e):
                replay(e, ops["pool"], finals)

            @block.sync
            def _(e):
                replay(e, ops["sp"], finals + tail)
        self.ops = {e: [] for e in self.ENG}
        for e in self.ENG:
            for f in self.CE:
                self.known[e][f] = self.cnt[f]
            self.kdma[e] = set(range(max(0, self.ndma - 2 * NDMA_SEM), self.ndma))

    def mm(self, out_t, reads, out_ap, pairs):
        n = len(pairs)

        def fn(e):
            ins = None
            for i, (l, r) in enumerate(pairs):
                ins = e.matmul(out_ap, l, r, start=(i == 0), stop=(i == n - 1))
            return ins
        self.op("pe", [out_t], reads, fn)

    def mm1(self, out_t, reads, out_ap, l, r, start, stop):
        self.op("pe", [out_t], reads, lambda e: e.matmul(out_ap, l, r, start=start, stop=stop))

    def transposes(self, out_t, reads, items, ident_ap):
        def fn(e):
            ins = None
            for (o, i) in items:
                ins = e.transpose(o, i, ident_ap)
            return ins
        self.op("pe", [out_t], reads, fn)

    def act(self, out_t, reads, out_ap, in_ap, func, scale=1.0, bias=0.0):
        self.op("act", [out_t], reads,
                lambda e: e.activation(out=out_ap, in_=in_ap, func=func, scale=scale, bias=bias))

    def tt(self, out_t, reads, out_ap, a_ap, b_ap, op, eng="dve"):
        self.op(eng, [out_t], reads, lambda e: e.tensor_tensor(out=out_ap, in0=a_ap, in1=b_ap, op=op))

    def ts(self, out_t, reads, out_ap, in_ap, s1, op0, s2=None, op1=None, eng="dve"):
        if op1 is None:
            self.op(eng, [out_t], reads,
                    lambda e: e.tensor_scalar(out=out_ap, in0=in_ap, scalar1=s1, scalar2=None, op0=op0))
        else:
            self.op(eng, [out_t], reads,
                    lambda e: e.tensor_scalar(out=out_ap, in0=in_ap, scalar1=s1, scalar2=s2, op0=op0, op1=op1))

    def stt(self, out_t, reads, out_ap, in0, scalar, in1, op0, op1, eng="dve"):
        self.op(eng, [out_t], reads,
                lambda e: e.scalar_tensor_tensor(out=out_ap, in0=in0, scalar=scalar, in1=in1, op0=op0, op1=op1))

    def copy(self, out_t, reads, out_ap, in_ap, eng="dve"):
        self.op(eng, [out_t], reads, lambda e: e.tensor_copy(out=out_ap, in_=in_ap))

    def memset(self, out_t, out_ap, val, eng="pool"):
        self.op(eng, [out_t], [], lambda e: e.memset(out_ap, val))


class SB:
    _n = [0]

    def __init__(self, st, nc, name, shape, dtype, psum=False):
        SB._n[0] += 1
        name = "sb%d_%s" % (SB._n[0], name)
        if psum:
            self.t = st.enter_context(nc.psum_tensor(name, shape, dtype))
        else:
            self.t = st.enter_context(nc.sbuf_tensor(name, shape, dtype))
        self.tile = Tile(name, psum=psum)

    def __getitem__(self, k):
        return self.t[k]


def build(nlayers=L, debug=False):
    nc = bass.Bass("TRN2", target_bir_lowering=False)
    okind = "ExternalOutput" if debug else None

    def din(name, shape, dt=F32):
        return nc.dram_tensor(name, list(shape), dt, kind="ExternalInput").ap()

    def dscr(name, shape, dt):
        if okind:
            return nc.dram_tensor(name, list(shape), dt, kind=okind).ap()
        return nc.dram_tensor(name, list(shape), dt).ap()

    x_d = din("x", [S, D]); ctx_d = din("ctx", [CT, D]); cvec_d = din("cvec", [128, 8, 2])
    wmod_d = din("w_mod", [L, D, 6 * D]); bmod_d = din("bmod", [128, L, 48])
    ng_d = din("ng", [128, L, 2, 8]); fng_d = din("fng", [128, 8])
    win_d = din("w_in", [L, D, INW]); wkrp_d = din("w_krp", [L, D, 32])
    convw_d = din("convw", [128, L, 3, 4]); lcw_d = din("lcw", [128, L, 4, 4]); lcb_d = din("lcb", [128, L, 4])
    lwa_d = din("lwa", [128, L, 2, 2, 4, 128]); lb_d = din("lb", [128, L, 3, 2, 4])
    clg_d = din("clg", [128, L, 2, 4]); cws_d = din("cws", [128, L, 4, 128]); cbs_d = din("cbs", [128, L, 4, 128])
    qng_d = din("qng", [128, L, 3]); kvng_d = din("kvng", [128, L, 2])
    wq_d = din("wq", [L, 384, 768]); wqp_d = din("wqp", [L, 384, 768])
    wk_d = din("wk", [L, 256, 512]); wv_d = din("wv", [L, 256, 512])
    wbr_d = din("w_branch", [L, 4, 512, D]); wout_d = din("w_out", [L, D, D])
    wf1_d = din("w_ff1", [L, D, DFF]); wf3_d = din("w_ff3", [L, D, DFF]); wf2_d = din("w_ff2", [L, DFF, D])
    rope_d = din("rope", [128, 2, S]); ident_d = din("ident", [128, 128])
    out_d = nc.dram_tensor("out", [S, D], F32, kind="ExternalOutput").ap()

    xT_d = dscr("xT", [D, T], F32)
    hT_d = dscr("hT", [D, T], BF16)
    lrux_d = dscr("p_lrux", [512, T], BF16); kvl_d = dscr("p_kvl", [256, T], BF16)
    lrug_d = dscr("p_lrug", [512, T], BF16); ql_d = dscr("p_ql", [384, T], BF16)
    cb_d = dscr("p_cb", [512, T], BF16); cc_d = dscr("p_cc", [512, T], BF16); cx_d = dscr("p_cx", [512, T], BF16)
    cu_d = dscr("p_cu", [512, T], BF16); gate_d = dscr("p_gate", [4096, T], BF16)
    kr_d = dscr("p_kr", [32, T], BF16)
    yall_d = dscr("yall", [2048, T], BF16)
    ffh_d = dscr("ffh", [DFF, T], BF16)

    def fm(ap, p=128):
        return ap.rearrange("(c p) t -> p c t", p=p)

    with ExitStack() as gst:
        P = Prog(nc, gst)
        PS = [SB(gst, nc, "ps%d" % i, [128, 512], F32, psum=True) for i in range(8)]
        ident = SB(gst, nc, "ident", [128, 128], F32)
        ones_bf = SB(gst, nc, "ones_bf", [128, 128], BF16)
        ones_f = SB(gst, nc, "ones_f", [128, 64], F32)
        MOD = SB(gst, nc, "MOD", [128, L, 48, 2], F32)
        AM = SB(gst, nc, "AM", [128, L, 2, 8, 2], F32)
        ngt = SB(gst, nc, "ngt", [128, L, 2, 8], F32)
        fng = SB(gst, nc, "fng", [128, 8], F32)
        convw = SB(gst, nc, "convw", [128, L, 3, 4], F32)
        lcw = SB(gst, nc, "lcw", [128, L, 4, 4], F32)
        lcb = SB(gst, nc, "lcb", [128, L, 4], F32)
        lb = SB(gst, nc, "lb", [128, L, 3, 2, 4], F32)
        lsc = SB(gst, nc, "lsc", [128, L, 2, 2, 4], F32)
        clg = SB(gst, nc, "clg", [128, L, 2, 4], F32)
        qng = SB(gst, nc, "qng", [128, L, 3], F32)
        kvng = SB(gst, nc, "kvng", [128, L, 2], F32)
        zero8 = SB(gst, nc, "zero8", [128, 8], F32)

        consts = [ident, ones_bf, ones_f, MOD, AM, ngt, fng, convw, lcw, lcb, lb, lsc, clg, qng, kvng, zero8]

        P.dma([ident.tile], [], ident[:], ident_d[:])
        P.memset(ones_bf.tile, ones_bf[:], 1.0)
        P.memset(ones_f.tile, ones_f[:], 1.0)
        P.memset(zero8.tile, zero8[:], 0.0)
        for (sbt, dd) in ((ngt, ng_d), (fng, fng_d), (convw, convw_d), (lcw, lcw_d), (lcb, lcb_d), (lb, lb_d),
                          (clg, clg_d), (qng, qng_d), (kvng, kvng_d)):
            P.dma([sbt.tile], [], sbt[:], dd[:])
        lam_ap = lb[:, :, 2, :, :]
        P.act(lsc.tile, [lb.tile], lsc[:, :, 0, :, :], lam_ap, AF.Exp, scale=-1.0)
        P.act(lsc.tile, [lsc.tile], lsc[:, :, 0, :, :], lsc[:, :, 0, :, :], AF.Ln, scale=1.0, bias=1.0)
        P.ts(lsc.tile, [lsc.tile], lsc[:, :, 1, :, :], lsc[:, :, 0, :, :], -16.0, ALU.mult)
        P.ts(lsc.tile, [lsc.tile], lsc[:, :, 0, :, :], lsc[:, :, 0, :, :], -8.0, ALU.mult)
        P.flush()

        with ExitStack() as st:
            xin = [SB(st, nc, "xin%d" % i, [128, 4, D], F32) for i in range(2)]
            xts = [SB(st, nc, "xts%d" % i, [128, 8, 512], F32) for i in range(2)]
            for ti, (t0, w) in enumerate(TILES):
                b = ti % 2
                nsub = w // 128
                src = ctx_d if ti == 0 else x_d
                r0 = 0 if ti == 0 else t0 - CT
                P.dma([xin[b].tile], [], xin[b][:, 0:nsub, :], src[r0:r0 + w, :].rearrange("(a p) d -> p a d", p=128))
                for c in range(8):
                    ps = PS[c]
                    items = [(ps[:, a * 128:(a + 1) * 128], xin[b][:, a, c * 128:(c + 1) * 128]) for a in range(nsub)]
                    P.transposes(ps.tile, [xin[b].tile, ident.tile], items, ident[:])
                    if c % 2 == 0:
                        P.copy(xts[b].tile, [ps.tile], xts[b][:, c, 0:w], ps[:, 0:w], eng="dve")
                    else:
                        P.act(xts[b].tile, [ps.tile], xts[b][:, c, 0:w], ps[:, 0:w], AF.Copy)
                P.dma([], [xts[b].tile], fm(xT_d)[:, :, t0:t0 + w], xts[b][:, :, 0:w])
            P.flush()

        with ExitStack() as st:
            cv = SB(st, nc, "cv", [128, 8, 2], F32)
            sv = SB(st, nc, "sv", [128, 8, 2], F32)
            bm = SB(st, nc, "bm", [128, L, 48], F32)
            wst = [SB(st, nc, "wmst%d" % i, [128, 8, 768], F32) for i in range(2)]
            wmb = [SB(st, nc, "wmb%d" % i, [128, 8, 768], BF16) for i in range(2)]
            svb = SB(st, nc, "svb", [128, 8, 2], BF16)
            P.dma([cv.tile], [], cv[:], cvec_d[:])
            P.dma([bm.tile], [], bm[:], bmod_d[:])
            P.act(sv.tile, [cv.tile], sv[:], cv[:], AF.Silu)
            P.copy(svb.tile, [sv.tile], svb[:], sv[:], eng="dve")
            k = 0
            for l in range(nlayers):
                for sl in range(8):
                    b = k % 2
                    P.dma([wst[b].tile], [], wst[b][:], wmod_d[l, :, sl * 768:(sl + 1) * 768].rearrange("(kc p) n -> p kc n", p=128))
                    if k % 3 == 0:
                        P.copy(wmb[b].tile, [wst[b].tile], wmb[b][:], wst[b][:], eng="dve")
                    elif k % 3 == 1:
                        P.copy(wmb[b].tile, [wst[b].tile], wmb[b][:], wst[b][:], eng="pool")
                    else:
                        P.act(wmb[b].tile, [wst[b].tile], wmb[b][:], wst[b][:], AF.Copy)
                    ps = PS[k % 4]
                    for jj in range(6):
                        P.mm(ps.tile, [wmb[b].tile, svb.tile], ps[:, jj * 2:jj * 2 + 2],
                             [(wmb[b][:, kc, jj * 128:(jj + 1) * 128], svb[:, kc, :]) for kc in range(8)])
                    for s in range(2):
                        P.tt(MOD.tile, [ps.tile, bm.tile], MOD[:, l, sl * 6:sl * 6 + 6, s],
                             ps[:, 0:12].rearrange("p (j s) -> p j s", s=2)[:, :, s], bm[:, l, sl * 6:sl * 6 + 6], ALU.add)
                    k += 1
                for which in range(2):
                    for s in range(2):
                        sc = MOD[:, l, (8 + 24 * which):(16 + 24 * which), s]
                        P.stt(AM.tile, [MOD.tile, ngt.tile], AM[:, l, which, :, s], sc, 1.0, ngt[:, l, which, :], ALU.add, ALU.mult)
            P.flush()

        def modv(l, m, c, s):
            return MOD[:, l, m * 8 + c, s:s + 1]

        def norm_tile(x_sb, w, sq, rstd, tmps, psn, out_fn, out_tiles, a_fn, b_fn):
            P.act(sq.tile, [x_sb.tile], sq[:, :, 0:w], x_sb[:, :, 0:w], AF.Square)
            P.mm(psn.tile, [sq.tile, ones_bf.tile], psn[:, 0:w], [(ones_bf[:], sq[:, c, 0:w]) for c in range(8)])
            P.act(rstd.tile, [psn.tile], rstd[:, 0:w], psn[:, 0:w], AF.Sqrt, scale=1.0 / D, bias=EPS)
            P.op("dve", [rstd.tile], [rstd.tile], lambda e: e.reciprocal(out=rstd[:, 0:w], in_=rstd[:, 0:w]))
            for c in range(8):
                tm = tmps[c % 2]
                P.tt(tm.tile, [x_sb.tile, rstd.tile], tm[:, 0:w], x_sb[:, c, 0:w], rstd[:, 0:w], ALU.mult)
                P.act(out_tiles[c] if isinstance(out_tiles, list) else out_tiles, [tm.tile, MOD.tile, AM.tile],
                      out_fn(c), tm[:, 0:w], AF.Identity, scale=a_fn(c), bias=b_fn(c))

        dbg = {}
        for l in range(nlayers):
            last = (l == L - 1)
            tiles_l = [ti for ti in range(9) if not (last and ti == 0)]

            with ExitStack() as st1:
                hT = SB(st1, nc, "hT", [128, 8, T], BF16)
                hTt = [Tile("hT%d" % i) for i in range(9)]
                with ExitStack() as st:
                    xt = [SB(st, nc, "xt%d" % i, [128, 8, 512], F32) for i in range(2)]
                    sqs = [SB(st, nc, "sq%d" % i, [128, 8, 512], BF16) for i in range(2)]
                    rstds = [SB(st, nc, "rstd%d" % i, [128, 512], F32) for i in range(2)]
                    tmps = [SB(st, nc, "ntmp%d" % i, [128, 512], F32) for i in range(4)]
                    wst = [SB(st, nc, "wst%d" % i, [128, 8, 512], F32) for i in range(2)]
                    wbf = [SB(st, nc, "wbf%d" % i, [128, 8, 512], BF16) for i in range(2)]
                    ostg = [SB(st, nc, "ostg%d" % i, [128, 4, 512], BF16) for i in range(3)]
                    def n1_load(ti):
                        t0, w = TILES[ti]
                        b = ti % 2
                        P.dma([xt[b].tile], [], xt[b][:, :, 0:w], fm(xT_d)[:, :, t0:t0 + w])

                    def n1_norm(ti):
                        t0, w = TILES[ti]
                        b = ti % 2
                        s = 1 if ti == 0 else 0
                        norm_tile(xt[b], w, sqs[b], rstds[b], tmps[2 * b:2 * b + 2], PS[6 + b],
                                  lambda c: hT[:, c, t0:t0 + w], hTt[ti],
                                  lambda c: AM[:, l, 0, c, s:s + 1], lambda c: modv(l, 0, c, s))
                    n1_load(0)
                    n1_load(1)
                    n1_norm(0)
                    regions = [(C_LRUX, 512, None, lrux_d, False), (C_KV, 256, None, kvl_d, False),
                               (C_LRUG, 512, AF.Gelu_apprx_tanh, lrug_d, True), (C_QL, 384, None, ql_d, True),
                               (C_CB, 512, None, cb_d, True), (C_CC, 512, None, cc_d, True), (C_CX, 512, None, cx_d, True),
                               (C_CU, 512, AF.Gelu_apprx_tanh, cu_d, True)]
                    for g in range(8):
                        regions.append((C_GATE + 512 * g, 512, AF.Sigmoid, gate_d[512 * g:512 * (g + 1), :], True))
                    slabs = regions

                    def load_slab(i):
                        c0, n, fn_, dst, skipc = slabs[i]
                        b = i % 2
                        P.dma([wst[b].tile], [], wst[b][:, :, 0:n], win_d[l, :, c0:c0 + n].rearrange("(kc p) n -> p kc n", p=128))
                        P.copy(wbf[b].tile, [wst[b].tile], wbf[b][:, :, 0:n], wst[b][:, :, 0:n], eng="pool")
                    load_slab(0)
                    load_slab(1)
                    evc = [0]

                    def slab_tile(i, ti):
                        c0, n, fn_, dst, skipc = slabs[i]
                        b = i % 2
                        nch = n // 128
                        t0, w = TILES[ti]
                        if last and ti == 0 and skipc:
                            return
                        og = ostg[evc[0] % 3]
                        for j in range(nch):
                            ps = PS[evc[0] % 6]
                            evc[0] += 1
                            P.mm(ps.tile, [wbf[b].tile, hTt[ti]], ps[:, 0:w],
                                 [(wbf[b][:, kc, j * 128:(j + 1) * 128], hT[:, kc, t0:t0 + w]) for kc in range(8)])
                            if fn_ is not None:
                                P.act(og.tile, [ps.tile], og[:, j, 0:w], ps[:, 0:w], fn_)
                            elif evc[0] % 2 == 0:
                                P.act(og.tile, [ps.tile], og[:, j, 0:w], ps[:, 0:w], AF.Copy)
                            else:
                                P.copy(og.tile, [ps.tile], og[:, j, 0:w], ps[:, 0:w], eng="dve")
                        P.dma([], [og.tile], fm(dst)[:, :, t0:t0 + w], og[:, 0:nch, 0:w], q="act")
                    for ti in range(9):
                        if ti + 1 < 9:
                            if ti + 2 < 9:
                                n1_load(ti + 2)
                            n1_norm(ti + 1)
                        slab_tile(0, ti)
                        slab_tile(1, ti)
                    load_slab(2)
                    for i in range(2, len(slabs)):
                        if i + 1 < len(slabs):
                            load_slab(i + 1)
                        for ti in range(9):
                            slab_tile(i, ti)
                    P.flush()
                with ExitStack() as st:
                    rope = SB(st, nc, "rope", [128, 2, S], F32)
                    P.dma([rope.tile], [], rope[:], rope_d[:])
                    wkr_st = SB(st, nc, "wkr_st", [128, 8, 64], F32)
                    wkr = SB(st, nc, "wkr", [128, 8, 64], BF16)
                    krt = [SB(st, nc, "krt%d" % i, [32, 512], F32) for i in range(2)]
                    kro = [SB(st, nc, "kro%d" % i, [32, 512], BF16) for i in range(2)]
                    P.dma([wkr_st.tile], [], wkr_st[:, :, 0:32], win_d[l, :, C_KR:C_KR + 32].rearrange("(kc p) n -> p kc n", p=128))
                    P.dma([wkr_st.tile], [], wkr_st[:, :, 32:64], wkrp_d[l].rearrange("(kc p) n -> p kc n", p=128))
                    P.copy(wkr.tile, [wkr_st.tile], wkr[:], wkr_st[:], eng="pool")
                    for ti, (t0, w) in enumerate(TILES):
                        pa, pb = PS[0], PS[1]
                        P.mm(pa.tile, [wkr.tile, hT.tile], pa[0:32, 0:w], [(wkr[:, kc, 0:32], hT[:, kc, t0:t0 + w]) for kc in range(8)])
                        ko = kro[ti % 2]
                        if ti == 0:
                            P.copy(ko.tile, [pa.tile], ko[:, 0:w], pa[0:32, 0:w], eng="dve")
                        else:
                            P.mm(pb.tile, [wkr.tile, hT.tile], pb[0:32, 0:w], [(wkr[:, kc, 32:64], hT[:, kc, t0:t0 + w]) for kc in range(8)])
                            s0 = t0 - CT
                            P.tt(krt[0].tile, [pa.tile, rope.tile], krt[0][:, 0:w], pa[0:32, 0:w], rope[0:32, 0, s0:s0 + w], ALU.mult)
                            P.tt(krt[1].tile, [pb.tile, rope.tile], krt[1][:, 0:w], pb[0:32, 0:w], rope[0:32, 1, s0:s0 + w], ALU.mult)
                            P.tt(ko.tile, [krt[0].tile, krt[1].tile], ko[:, 0:w], krt[0][:, 0:w], krt[1][:, 0:w], ALU.add, eng="pool")
                        P.dma([], [ko.tile], kr_d[:, t0:t0 + w], ko[:, 0:w])
                    P.flush()

                with ExitStack() as st:
                    wst = SB(st, nc, "cwst", [128, 8, 512], F32)
                    wv = SB(st, nc, "cwv", [128, 8, 512], BF16)
                    ws_f = SB(st, nc, "ws_f", [128, 4, 128], F32)
                    ws_b = SB(st, nc, "ws_b", [128, 4, 128], BF16)
                    bs_bc = SB(st, nc, "bs_bc", [128, 4, 128], F32)
                    Rg = SB(st, nc, "Rg", [128, 4, 128], F32)
                    vg = [SB(st, nc, "vg%d" % i, [128, 512], F32) for i in range(3)]
                    vn = [SB(st, nc, "vn%d" % i, [128, 512], BF16) for i in range(3)]
                    stt_ = [SB(st, nc, "bst%d" % i, [128, 8], F32) for i in range(3)]
                    ut = [SB(st, nc, "ut%d" % i, [128, 4, 512], BF16) for i in range(2)]
                    yc = [SB(st, nc, "yc%d" % i, [128, 4, 512], BF16) for i in range(2)]
                    tq = [SB(st, nc, "tq%d" % i, [128, 4, 128], F32) for i in range(2)]
                    P.dma([wst.tile], [], wst[:], win_d[l, :, C_CV:C_CV + 512].rearrange("(kc p) n -> p kc n", p=128))
                    P.copy(wv.tile, [wst.tile], wv[:], wst[:], eng="pool")
                    P.dma([ws_f.tile], [], ws_f[:], cws_d[:, l, :, :])
                    P.dma([bs_bc.tile], [], bs_bc[:], cbs_d[:, l, :, :])
                    P.copy(ws_b.tile, [ws_f.tile], ws_b[:], ws_f[:], eng="dve")
                    for g in range(4):
                        P.mm(PS[6].tile, [ones_bf.tile, ws_b.tile], PS[6][:, g * 128:(g + 1) * 128], [(ones_bf[:], ws_b[:, g, :])])
                        P.stt(Rg.tile, [PS[6].tile, clg.tile, bs_bc.tile], Rg[:, g, :], PS[6][:, g * 128:(g + 1) * 128],
                              clg[:, l, 1, g:g + 1], bs_bc[:, g, :], ALU.mult, ALU.add)
                    chunks = [(ti, a) for ti in tiles_l for a in range(TILES[ti][1] // 128)]

                    def cA1(q):
                        ti, a = chunks[q]
                        t0, w = TILES[ti]
                        b = ti % 2
                        if a == 0:
                            P.dma([ut[b].tile], [], ut[b][:, :, 0:w], fm(cu_d)[:, :, t0:t0 + w])
                        q0 = t0 + a * 128
                        pv = PS[q % 3]
                        vb = q % 3
                        sb_ = stt_[vb]
                        P.mm(pv.tile, [hT.tile, wv.tile], pv[:, :], [(hT[:, kc, q0:q0 + 128], wv[:, kc, :]) for kc in range(8)])
                        P.act(vg[vb].tile, [pv.tile], vg[vb][:], pv[:, :], AF.Gelu_apprx_tanh)
                        P.op("dve", [sb_.tile], [vg[vb].tile], lambda e, sb_=sb_, vb=vb: e.bn_stats(out=sb_[:, 0:6], in_=vg[vb][:]))
                        P.op("dve", [sb_.tile], [sb_.tile], lambda e, sb_=sb_: e.bn_aggr(out=sb_[:, 6:8], in_=sb_[:, 0:6]))
                        P.act(sb_.tile, [sb_.tile], sb_[:, 7:8], sb_[:, 7:8], AF.Sqrt, scale=1.0, bias=EPS)

                    def cA2(q):
                        vb = q % 3
                        sb_ = stt_[vb]
                        P.op("dve", [sb_.tile], [sb_.tile], lambda e, sb_=sb_: e.reciprocal(out=sb_[:, 7:8], in_=sb_[:, 7:8]))
                        P.ts(vn[vb].tile, [vg[vb].tile, sb_.tile], vn[vb][:], vg[vb][:], sb_[:, 6:7], ALU.subtract, sb_[:, 7:8], ALU.mult)

                    def cB(q):
                        ti, a = chunks[q]
                        t0, w = TILES[ti]
                        b = ti % 2
                        vb = q % 3
                        pm = PS[3 + q % 2]
                        for g in range(4):
                            P.mm(pm.tile, [vn[vb].tile, ws_b.tile], pm[:, g * 128:(g + 1) * 128], [(vn[vb][:, g * 128:(g + 1) * 128], ws_b[:, g, :])])
                        for g in range(4):
                            P.stt(tq[q % 2].tile, [pm.tile, clg.tile, Rg.tile], tq[q % 2][:, g, :], pm[:, g * 128:(g + 1) * 128],
                                  clg[:, l, 0, g:g + 1], Rg[:, g, :], ALU.mult, ALU.add)
                        P.tt(yc[b].tile, [tq[q % 2].tile, ut[b].tile], yc[b][:, :, a * 128:(a + 1) * 128], tq[q % 2][:],
                             ut[b][:, :, a * 128:(a + 1) * 128], ALU.mult, eng="pool")
                        if a == w // 128 - 1:
                            P.dma([], [yc[b].tile], fm(yall_d[1024:1536, :])[:, :, t0:t0 + w], yc[b][:, :, 0:w], q="act")
                    cA1(0)
                    cA1(1)
                    for q in range(len(chunks)):
                        if q + 2 < len(chunks):
                            cA1(q + 2)
                        cA2(q)
                        cB(q)
                    P.flush()

            with ExitStack() as st:
                Bt = [SB(st, nc, "Bt%d" % i, [128, T], BF16) for i in range(2)]
                Ct = [SB(st, nc, "Ct%d" % i, [128, T], BF16) for i in range(2)]
                Xt = [SB(st, nc, "Xt%d" % i, [128, T], BF16) for i in range(2)]
                cxb = [SB(st, nc, "cxb%d" % i, [128, T + 4], BF16) for i in range(2)]
                ya = [SB(st, nc, "ya%d" % i, [128, T], BF16) for i in range(2)]
                dwa = [SB(st, nc, "dwa%d" % i, [128, 3, 128], BF16) for i in range(2)]
                a0 = 0 if not last else CT
                cxt = [[Tile("cxt%d_%d" % (i, j)) for j in range(3)] for i in range(2)]
                for i in range(2):
                    P.memset(cxb[i].tile, cxb[i][:], 0.0)

                def ca_load(c):
                    b = c % 2
                    rows = slice(c * 128, (c + 1) * 128)
                    P.dma([Bt[b].tile], [], Bt[b][:, a0:T], cb_d[rows, a0:T])
                    P.dma([Ct[b].tile], [], Ct[b][:, a0:T], cc_d[rows, a0:T])
                    P.dma([Xt[b].tile], [], Xt[b][:, a0:T], cx_d[rows, a0:T])
                ca_load(0)
                ev = 0
                for c in range(4):
                    b = c % 2
                    if c + 1 < 4:
                        ca_load(c + 1)
                    for k in range(3):
                        P.ts(dwa[b].tile, [ident.tile, convw.tile], dwa[b][:, k, :], ident[:], convw[:, l, k, c:c + 1], ALU.mult)
                    if not last:
                        P.tt(cxt[b][0], [cxb[b].tile, Ct[b].tile, Xt[b].tile], cxb[b][:, 1:257], Ct[b][:, 0:256], Xt[b][:, 0:256], ALU.mult, eng="pool")
                    P.tt(cxt[b][1], [cxb[b].tile, Ct[b].tile, Xt[b].tile], cxb[b][:, 259:259 + 2048], Ct[b][:, 256:256 + 2048], Xt[b][:, 256:256 + 2048], ALU.mult)
                    P.tt(cxt[b][2], [cxb[b].tile, Ct[b].tile, Xt[b].tile], cxb[b][:, 259 + 2048:259 + 4096], Ct[b][:, 256 + 2048:T], Xt[b][:, 256 + 2048:T], ALU.mult, eng="pool")
                    for ti in tiles_l:
                        t0, w = TILES[ti]
                        p0 = t0 + (1 if ti == 0 else 3)
                        ps = PS[ev % 6]
                        ev += 1
                        P.mm(ps.tile, [dwa[b].tile, cxb[b].tile] + cxt[b], ps[:, 0:w], [(dwa[b][:, k, :], cxb[b][:, p0 + k - 1:p0 + k - 1 + w]) for k in range(3)])
                        P.tt(ya[b].tile, [ps.tile, Bt[b].tile], ya[b][:, t0:t0 + w], ps[:, 0:w], Bt[b][:, t0:t0 + w], ALU.mult)
                    P.dma([], [ya[b].tile], yall_d[c * 128:(c + 1) * 128, a0:T], ya[b][:, a0:T])
                P.flush()

            with ExitStack() as st:
                xpbs = [SB(st, nc, "lxpb%d" % i, [128, T + 6], BF16) for i in range(2)]
                xc = SB(st, nc, "lxc", [128, T], F32)
                xcb = SB(st, nc, "lxcb", [128, T], BF16)
                Rr = [SB(st, nc, "lR%d" % i, [128, T], F32) for i in range(2)]
                Ii = [SB(st, nc, "lI%d" % i, [128, T], F32) for i in range(2)]
                Aa = [SB(st, nc, "lA%d" % i, [128, T], F32) for i in range(2)]
                gts = [SB(st, nc, "lg%d" % i, [128, T], BF16) for i in range(2)]
                yb = SB(st, nc, "lyb", [128, T], BF16)
                wa_b = SB(st, nc, "lwa_b", [128, 2, 2, 4, 128], BF16)
                dwl = SB(st, nc, "dwl", [128, 4, 128], BF16)
                with ExitStack() as st2:
                    wa_f = SB(st2, nc, "lwa_f", [128, 2, 2, 4, 128], F32)
                    P.dma([wa_f.tile], [], wa_f[:], lwa_d[:, l])
                    P.copy(wa_b.tile, [wa_f.tile], wa_b[:], wa_f[:], eng="pool")
                    for i in range(2):
                        P.memset(xpbs[i].tile, xpbs[i][:], 0.0)
                    P.flush()
                a0 = 0 if not last else CT

                def rv(tn, lo, hi):
                    return bass.AP(tn.t, hi - 1, [[T, 128], [-1, hi - lo]])

                def l_load(c):
                    rows = slice(c * 128, (c + 1) * 128)
                    xpb = xpbs[c % 2]
                    P.dma([xpb.tile], [], xpb[:, 2:258], lrux_d[rows, 0:CT])
                    P.dma([xpb.tile], [], xpb[:, 261:261 + S], lrux_d[rows, CT:T])
                    P.dma([gts[c % 2].tile], [], gts[c % 2][:, a0:T], lrug_d[rows, a0:T])
                ev = 0
                l_load(0)
                for c in range(4):
                    xpb = xpbs[c % 2]
                    gt = gts[c % 2]
                    for k in range(4):
                        P.ts(dwl.tile, [ident.tile, lcw.tile], dwl[:, k, :], ident[:], lcw[:, l, k, c:c + 1], ALU.mult)
                    for ti, (t0, w) in enumerate(TILES):
                        p0 = t0 + (2 if ti == 0 else 5)
                        ps = PS[ev % 6]
                        ev += 1
                        P.mm(ps.tile, [dwl.tile, xpb.tile], ps[:, 0:w], [(dwl[:, k, :], xpb[:, p0 + k - 2:p0 + k - 2 + w]) for k in range(4)])
                        P.act(xc.tile, [ps.tile, lcb.tile], xc[:, t0:t0 + w], ps[:, 0:w], AF.Identity, scale=1.0, bias=lcb[:, l, c:c + 1])
                        P.copy(xcb.tile, [xc.tile], xcb[:, t0:t0 + w], xc[:, t0:t0 + w], eng="dve")
                    if c + 1 < 4:
                        l_load(c + 1)
                    for d in range(2):
                        for ti, (t0, w) in enumerate(TILES):
                            pa = PS[ev % 6]
                            pb = PS[(ev + 1) % 6]
                            ev += 2
                            P.mm(pa.tile, [wa_b.tile, xcb.tile], pa[:, 0:w], [(wa_b[:, d, 0, c, :], xcb[:, t0:t0 + w])])
                            P.mm(pb.tile, [wa_b.tile, xcb.tile], pb[:, 0:w], [(wa_b[:, d, 1, c, :], xcb[:, t0:t0 + w])])
                            P.act(Rr[d].tile, [pa.tile, lb.tile], Rr[d][:, t0:t0 + w], pa[:, 0:w], AF.Sigmoid, scale=1.0, bias=lb[:, l, 0, d, c:c + 1])
                            P.act(Ii[d].tile, [pb.tile, lb.tile], Ii[d][:, t0:t0 + w], pb[:, 0:w], AF.Sigmoid, scale=1.0, bias=lb[:, l, 1, d, c:c + 1])
                    for d in range(2):
                        P.act(Aa[d].tile, [Rr[d].tile, lsc.tile], Aa[d][:], Rr[d][:], AF.Exp, scale=lsc[:, l, 0, d, c:c + 1])
                        P.act(Rr[d].tile, [Rr[d].tile, lsc.tile], Rr[d][:], Rr[d][:], AF.Exp, scale=lsc[:, l, 1, d, c:c + 1])
                        P.tt(Ii[d].tile, [Ii[d].tile, xc.tile], Ii[d][:], Ii[d][:], xc[:], ALU.mult, eng="pool")
                    for d in range(2):
                        P.ts(Rr[d].tile, [Rr[d].tile], Rr[d][:], Rr[d][:], 1.0, ALU.min, -1.0, ALU.mult)
                    for d in range(2):
                        P.act(Rr[d].tile, [Rr[d].tile], Rr[d][:], Rr[d][:], AF.Sqrt, scale=1.0, bias=1.0)
                    P.tt(Ii[0].tile, [Ii[0].tile, Rr[0].tile], Ii[0][:], Ii[0][:], Rr[0][:], ALU.mult)
                    P.op("dve", [Rr[0].tile], [Aa[0].tile, Ii[0].tile],
                         lambda e: e.tensor_tensor_scan(out=Rr[0][:], data0=Aa[0][:], data1=Ii[0][:], initial=0.0, op0=ALU.mult, op1=ALU.add))
                    P.tt(Ii[1].tile, [Ii[1].tile, Rr[1].tile], Ii[1][:], Ii[1][:], Rr[1][:], ALU.mult)
                    P.op("dve", [Rr[1].tile], [Aa[1].tile, Ii[1].tile],
                         lambda e: e.tensor_tensor_scan(out=rv(Rr[1], 0, CT), data0=rv(Aa[1], 0, CT), data1=rv(Ii[1], 0, CT),
                                                        initial=0.0, op0=ALU.mult, op1=ALU.add))
                    P.op("dve", [Rr[1].tile], [Aa[1].tile, Ii[1].tile, Rr[1].tile],
                         lambda e: e.tensor_tensor_scan(out=rv(Rr[1], CT, T), data0=rv(Aa[1], CT, T), data1=rv(Ii[1], CT, T),
                                                        initial=Rr[1][:, 0:1], op0=ALU.mult, op1=ALU.add))
                    P.tt(Rr[0].tile, [Rr[0].tile, Rr[1].tile], Rr[0][:, a0:T], Rr[0][:, a0:T], Rr[1][:, a0:T], ALU.add)
                    P.tt(yb.tile, [Rr[0].tile, gt.tile], yb[:, a0:T], Rr[0][:, a0:T], gt[:, a0:T], ALU.mult)
                    P.dma([], [yb.tile], yall_d[512 + c * 128:512 + (c + 1) * 128, a0:T], yb[:, a0:T])
                P.flush()

            with ExitStack() as st:
                KT = SB(st, nc, "KT", [128, 8, T], BF16)
                VA = SB(st, nc, "VA", [128, 34, 8, 65], BF16)
                rope = SB(st, nc, "ropeq", [128, 2, S], F32)
                wk = SB(st, nc, "mwk", [128, 2, 512], BF16); wv = SB(st, nc, "mwv", [128, 2, 512], BF16)
                wq = SB(st, nc, "mwq", [128, 3, 768], BF16); wqp = SB(st, nc, "mwqp", [128, 3, 768], BF16)
                stkv = ExitStack()
                wstg = SB(stkv, nc, "mwst", [128, 3, 768], F32)
                kvl = [SB(stkv, nc, "kvl%d" % i, [128, 2, 512], BF16) for i in range(2)]
                ksq = SB(stkv, nc, "ksq", [128, 3, 512], BF16)
                rkv = SB(stkv, nc, "rkv", [128, 512], F32)
                rv1 = [SB(stkv, nc, "rv1%d" % i, [128, 2], F32) for i in range(2)]
                P.dma([rope.tile], [], rope[:], rope_d[:])
                P.memset(VA.tile, VA[:], 1.0)
                for (dd, nk, ncol, dst, gsb) in ((wk_d, 2, 512, wk, kvng), (wv_d, 2, 512, wv, kvng), (wq_d, 3, 768, wq, qng), (wqp_d, 3, 768, wqp, qng)):
                    P.dma([wstg.tile], [], wstg[:, 0:nk, 0:ncol], dd[l].rearrange("(kc p) n -> p kc n", p=128))
                    for kc in range(nk):
                        P.ts(dst.tile, [wstg.tile, gsb.tile], dst[:, kc, :], wstg[:, kc, 0:ncol], gsb[:, l, kc:kc + 1], ALU.mult)
                ksq2 = SB(stkv, nc, "ksq2", [128, 2, 512], BF16)
                rkv2 = SB(stkv, nc, "rkv2", [128, 512], F32)
                ksqs = [ksq, ksq2]
                rkvs = [rkv, rkv2]
                r4 = [SB(stkv, nc, "r4%d" % i, [128, 8], F32) for i in range(2)]

                KTr = Tile("KTr")

                def kv_pro(ti):
                    t0, w = TILES[ti]
                    b = ti % 2
                    kv = kvl[b]
                    P.dma([kv.tile], [], kv[:, :, 0:w], fm(kvl_d)[:, :, t0:t0 + w])
                    P.tt(ksqs[b].tile, [kv.tile], ksqs[b][:, 0:2, 0:w], kv[:, :, 0:w], kv[:, :, 0:w], ALU.mult, eng="pool")
                    P.mm(PS[7].tile, [ksqs[b].tile, ones_bf.tile], PS[7][:, 0:w], [(ones_bf[:], ksqs[b][:, kc, 0:w]) for kc in range(2)])
                    P.act(rkvs[b].tile, [PS[7].tile], rkvs[b][:, 0:w], PS[7][:, 0:w], AF.Sqrt, scale=1.0 / 256, bias=EPS)
                    P.op("dve", [rkvs[b].tile], [rkvs[b].tile], lambda e, w=w, b=b: e.reciprocal(out=rkvs[b][:, 0:w], in_=rkvs[b][:, 0:w]))
                    na = w // 128

                    def fn(e, b=b, na=na):
                        ins = None
                        for a in range(na):
                            for kc in range(2):
                                ins = e.matmul(PS[6][:, a:a + 1], ksqs[b][:, kc, a * 128:(a + 1) * 128], ones_bf[:, 0:1], start=(kc == 0), stop=(kc == 1))
                        return ins
                    P.op("pe", [PS[6].tile], [ksqs[b].tile, ones_bf.tile], fn)
                    P.act(r4[b].tile, [PS[6].tile], r4[b][:, 0:na], PS[6][:, 0:na], AF.Sqrt, scale=1.0 / 256, bias=EPS)
                    P.op("dve", [r4[b].tile], [r4[b].tile], lambda e, b=b, na=na: e.reciprocal(out=r4[b][:, 4:4 + na], in_=r4[b][:, 0:na]))

                def kv_body(ti):
                    t0, w = TILES[ti]
                    b = ti % 2
                    kv = kvl[b]
                    rk = rkvs[b]
                    for h in range(8):
                        ps = PS[h % 4]
                        P.mm(ps.tile, [wk.tile, kv.tile], ps[0:64, 0:w], [(wk[:, kc, h * 64:(h + 1) * 64], kv[:, kc, 0:w]) for kc in range(2)])
                        P.tt(KT.tile, [ps.tile, rk.tile], KT[0:64, h, t0:t0 + w], ps[0:64, 0:w], rk[0:64, 0:w], ALU.mult)
                        P.dma([KTr], [], KT[64:96, h, t0:t0 + w], kr_d[:, t0:t0 + w])
                    for a in range(w // 128):
                        ch = (t0 + a * 128) // 128
                        pv = PS[4 + a % 2]
                        P.mm(pv.tile, [kv.tile, wv.tile], pv[:, :], [(kv[:, kc, a * 128:(a + 1) * 128], wv[:, kc, :]) for kc in range(2)])
                        P.ts(VA.tile, [pv.tile, r4[b].tile], VA[:, ch, :, 0:64], pv[:, :].rearrange("p (h d) -> p h d", d=64), r4[b][:, 4 + a:5 + a], ALU.mult)
                kv_pro(0)
                for ti in range(9):
                    if ti + 1 < 9:
                        kv_pro(ti + 1)
                    kv_body(ti)
                P.flush()
                stkv.close()
                qlt = [SB(st, nc, "qlt%d" % i, [128, 3, 512], BF16) for i in range(2)]
                Qh = [SB(st, nc, "Qh%d" % i, [128, 512], BF16) for i in range(2)]
                qtmp = [SB(st, nc, "qtmp%d" % i, [128, 512], F32) for i in range(2)]
                PT = [SB(st, nc, "PT%d" % i, [128, 512], BF16) for i in range(4)]
                rd = SB(st, nc, "rd", [128, 512], F32)
                bcs = SB(st, nc, "bcs", [64, 512], F32)
                yd = [SB(st, nc, "yd%d" % i, [64, 8, 512], BF16) for i in range(2)]
                rq = [SB(st, nc, "rq%d" % i, [128, 512], F32) for i in range(2)]
                qsq = [SB(st, nc, "qsq%d" % i, [128, 3, 512], BF16) for i in range(2)]
                jobs = [(ti, h) for ti in tiles_l for h in range(8)]
                tpos = {ti: n for n, ti in enumerate(tiles_l)}

                def prologue(ti):
                    t0, w = TILES[ti]
                    b = tpos[ti] % 2
                    ql = qlt[b]
                    P.dma([ql.tile], [], ql[:, :, 0:w], fm(ql_d)[:, :, t0:t0 + w])
                    P.tt(qsq[b].tile, [ql.tile], qsq[b][:, :, 0:w], ql[:, :, 0:w], ql[:, :, 0:w], ALU.mult, eng="pool")
                    P.mm(PS[7].tile, [qsq[b].tile, ones_bf.tile], PS[7][:, 0:w], [(ones_bf[:], qsq[b][:, kc, 0:w]) for kc in range(3)])
                    P.act(rq[b].tile, [PS[7].tile], rq[b][:, 0:w], PS[7][:, 0:w], AF.Sqrt, scale=1.0 / 384, bias=EPS)
                    P.op("dve", [rq[b].tile], [rq[b].tile], lambda e, w=w, b=b: e.reciprocal(out=rq[b][:, 0:w], in_=rq[b][:, 0:w]))

                def qprep(n):
                    ti, h = jobs[n]
                    t0, w = TILES[ti]
                    b = tpos[ti] % 2
                    ql = qlt[b]
                    rr = rq[b]
                    p1, p2 = PS[5], PS[6]
                    Q = Qh[n % 2]
                    P.mm(p1.tile, [wq.tile, ql.tile], p1[0:96, 0:w], [(wq[:, kc, h * 96:(h + 1) * 96], ql[:, kc, 0:w]) for kc in range(3)])
                    P.tt(Q.tile, [p1.tile, rr.tile], Q[0:64, 0:w], p1[0:64, 0:w], rr[0:64, 0:w], ALU.mult)
                    if ti == 0:
                        P.tt(Q.tile, [p1.tile, rr.tile], Q[64:96, 0:w], p1[64:96, 0:w], rr[64:96, 0:w], ALU.mult)
                    else:
                        s0 = t0 - CT
                        P.mm(p2.tile, [wqp.tile, ql.tile], p2[0:96, 0:w], [(wqp[:, kc, h * 96:(h + 1) * 96], ql[:, kc, 0:w]) for kc in range(3)])
                        P.tt(qtmp[0].tile, [p1.tile, rope.tile], qtmp[0][64:96, 0:w], p1[64:96, 0:w], rope[64:96, 0, s0:s0 + w], ALU.mult)
                        P.tt(qtmp[1].tile, [p2.tile, rope.tile], qtmp[1][64:96, 0:w], p2[64:96, 0:w], rope[64:96, 1, s0:s0 + w], ALU.mult)
                        P.tt(qtmp[0].tile, [qtmp[0].tile, qtmp[1].tile], qtmp[0][64:96, 0:w], qtmp[0][64:96, 0:w], qtmp[1][64:96, 0:w], ALU.add, eng="pool")
                        P.tt(Q.tile, [qtmp[0].tile, rr.tile], Q[64:96, 0:w], qtmp[0][64:96, 0:w], rr[64:96, 0:w], ALU.mult, eng="pool")

                def epi1(n):
                    ti, h = jobs[n]
                    w = TILES[ti][1]
                    accp = PS[3 + n % 2]
                    P.op("dve", [rd.tile], [accp.tile], lambda e, accp=accp, w=w: e.reciprocal(out=rd[64:65, 0:w], in_=accp[64:65, 0:w]))

                def epi2(n):
                    ti, h = jobs[n]
                    t0, w = TILES[ti]
                    accp = PS[3 + n % 2]
                    ydt = yd[tpos[ti] % 2]
                    P.mm(PS[7].tile, [rd.tile, ones_f.tile], PS[7][0:64, 0:w], [(ones_f[64:65, 0:64], rd[64:65, 0:w])])
                    P.copy(bcs.tile, [PS[7].tile], bcs[:, 0:w], PS[7][0:64, 0:w], eng="dve")
                    P.tt(ydt.tile, [accp.tile, bcs.tile], ydt[:, h, 0:w], accp[0:64, 0:w], bcs[:, 0:w], ALU.mult)
                    if h == 7:
                        P.dma([], [ydt.tile], fm(yall_d[1536:2048, :], p=64)[:, :, t0:t0 + w], ydt[:, :, 0:w])

                LA = 2
                gi = 0
                prologue(jobs[0][0])
                qprep(0)
                pend = None
                for n, (ti, h) in enumerate(jobs):
                    t0, w = TILES[ti]
                    keys = list(range(0, 2)) if ti == 0 else list(range(0, 34))
                    nk = len(keys)
                    Q = Qh[n % 2]
                    accp = PS[3 + n % 2]

                    def qk(i):
                        kc = keys[i]
                        pss = PS[(gi + i) % 3]
                        P.mm1(pss.tile, [Q.tile], pss[:, 0:w], KT[0:96, h, kc * 128:(kc + 1) * 128], Q[0:96, 0:w], True, True)
                    for i in range(min(LA, nk)):
                        qk(i)
                    i_prep = min(8, nk - 1)
                    i_epi = min(14, nk - 1)
                    for i in range(nk):
                        kc = keys[i]
                        pss = PS[(gi + i) % 3]
                        pt = PT[(gi + i) % 4]
                        P.act(pt.tile, [pss.tile], pt[:, 0:w], pss[:, 0:w], AF.Exp, scale=SCALE)
                        if i + LA < nk:
                            qk(i + LA)
                        P.mm1(accp.tile, [pt.tile], accp[0:65, 0:w], VA[:, kc, h, 0:65], pt[:, 0:w], i == 0, i == nk - 1)
                        if i == i_prep and n + 1 < len(jobs):
                            if jobs[n + 1][0] != ti:
                                prologue(jobs[n + 1][0])
                            qprep(n + 1)
                        if i == i_epi and pend is not None:
                            epi2(pend)
                            pend = None
                    gi += nk
                    epi1(n)
                    pend = n
                epi2(pend)
                P.flush()

            with ExitStack() as st:
                wb = SB(st, nc, "wb", [128, 12, D], BF16)
                wbD = SB(st, nc, "wbD", [64, 8, D], BF16)
                wo = SB(st, nc, "wo", [128, 8, D], BF16)
                with ExitStack() as st2:
                    wstg = [SB(st2, nc, "ewst%d" % i, [128, 4, D], F32) for i in range(2)]
                    k = 0
                    for n in range(3):
                        b = k % 2; k += 1
                        P.dma([wstg[b].tile], [], wstg[b][:], wbr_d[l, n].rearrange("(kc p) n -> p kc n", p=128))
                        P.copy(wb.tile, [wstg[b].tile], wb[:, n * 4:(n + 1) * 4, :], wstg[b][:], eng="pool" if n % 2 else "dve")
                    for hh in range(2):
                        b = k % 2; k += 1
                        P.dma([wstg[b].tile], [], wstg[b][0:64, :, :], wbr_d[l, 3, hh * 256:(hh + 1) * 256, :].rearrange("(h p) n -> p h n", p=64))
                        P.copy(wbD.tile, [wstg[b].tile], wbD[:, hh * 4:(hh + 1) * 4, :], wstg[b][0:64, :, :], eng="pool" if hh else "dve")
                    for hh in range(2):
                        b = k % 2; k += 1
                        P.dma([wstg[b].tile], [], wstg[b][:], wout_d[l, hh * 512:(hh + 1) * 512, :].rearrange("(kc p) n -> p kc n", p=128))
                        P.copy(wo.tile, [wstg[b].tile], wo[:, hh * 4:(hh + 1) * 4, :], wstg[b][:], eng="pool" if hh else "dve")
                    P.flush()
                y3s = [SB(st, nc, "y3%d" % i, [128, 12, 512], BF16) for i in range(2)]
                yDs = [SB(st, nc, "yD%d" % i, [64, 8, 512], BF16) for i in range(2)]
                sg = [SB(st, nc, "sg%d" % i, [128, 4, 512], BF16) for i in range(3)]
                xt = [SB(st, nc, "ext%d" % i, [128, 8, 512], F32) for i in range(3)]
                mg = SB(st, nc, "mg", [128, 8, 512], BF16)
                mt = [SB(st, nc, "mt%d" % i, [128, 512], F32) for i in range(8)]
                sq = SB(st, nc, "esq", [128, 8, 512], BF16)
                rstd = SB(st, nc, "erstd", [128, 512], F32)
                tmps = [SB(st, nc, "etmp%d" % i, [128, 512], F32) for i in range(2)]
                h2s = SB(st, nc, "h2s", [128, 8, 512], BF16)
                tp4 = {ti: n for n, ti in enumerate(tiles_l)}

                def e_load(ti):
                    t0, w = TILES[ti]
                    b = tp4[ti] % 2
                    bx = tp4[ti] % 3
                    P.dma([y3s[b].tile], [], y3s[b][:, :, 0:w], fm(yall_d[0:1536, :])[:, :, t0:t0 + w])
                    P.dma([yDs[b].tile], [], yDs[b][:, :, 0:w], fm(yall_d[1536:2048, :], p=64)[:, :, t0:t0 + w])
                    P.dma([xt[bx].tile], [], xt[bx][:, :, 0:w], fm(xT_d)[:, :, t0:t0 + w])

                def e_merge(ti):
                    t0, w = TILES[ti]
                    b = tp4[ti] % 2
                    y3, yD = y3s[b], yDs[b]
                    for j in range(8):
                        sgt = sg[j % 3]
                        P.dma([sgt.tile], [], sgt[:, :, 0:w],
                              gate_d.rearrange("(n j p) t -> p n j t", n=4, p=128)[:, :, j, t0:t0 + w])
                        mo = 4 * (j % 2)
                        for n in range(4):
                            ps = PS[mo + n]
                            if n < 3:
                                pairs = [(wb[:, n * 4 + kc, j * 128:(j + 1) * 128], y3[:, n * 4 + kc, 0:w]) for kc in range(4)]
                                P.mm(ps.tile, [wb.tile, y3.tile], ps[:, 0:w], pairs)
                            else:
                                pairs = [(wbD[0:64, h, j * 128:(j + 1) * 128], yD[0:64, h, 0:w]) for h in range(8)]
                                P.mm(ps.tile, [wbD.tile, yD.tile], ps[:, 0:w], pairs)
                            P.tt(mt[mo + n].tile, [ps.tile, sgt.tile], mt[mo + n][:, 0:w], ps[:, 0:w], sgt[:, n, 0:w], ALU.mult)
                        P.tt(mt[mo].tile, [mt[mo].tile, mt[mo + 1].tile], mt[mo][:, 0:w], mt[mo][:, 0:w], mt[mo + 1][:, 0:w], ALU.add, eng="pool")
                        P.tt(mt[mo + 2].tile, [mt[mo + 2].tile, mt[mo + 3].tile], mt[mo + 2][:, 0:w], mt[mo + 2][:, 0:w], mt[mo + 3][:, 0:w], ALU.add, eng="pool")
                        P.tt(mg.tile, [mt[mo].tile, mt[mo + 2].tile], mg[:, j, 0:w], mt[mo][:, 0:w], mt[mo + 2][:, 0:w], ALU.add, eng="pool")

                def e_outproj(ti):
                    t0, w = TILES[ti]
                    b = tp4[ti] % 3
                    s = 1 if ti == 0 else 0
                    for j in range(8):
                        ps = PS[j % 4]
                        P.mm(ps.tile, [wo.tile, mg.tile], ps[:, 0:w], [(wo[:, kc, j * 128:(j + 1) * 128], mg[:, kc, 0:w]) for kc in range(8)])
                        P.stt(xt[b].tile, [ps.tile, MOD.tile, xt[b].tile], xt[b][:, j, 0:w], ps[:, 0:w], modv(l, 2, j, s), xt[b][:, j, 0:w], ALU.mult, ALU.add)
                    P.dma([], [xt[b].tile], fm(xT_d)[:, :, t0:t0 + w], xt[b][:, :, 0:w], q="act")

                def e_norm(ti):
                    t0, w = TILES[ti]
                    b = tp4[ti] % 3
                    s = 1 if ti == 0 else 0
                    norm_tile(xt[b], w, sq, rstd, tmps, PS[7], lambda c: h2s[:, c, 0:w], h2s.tile,
                              lambda c: AM[:, l, 1, c, s:s + 1], lambda c: modv(l, 3, c, s))
                    P.dma([], [h2s.tile], fm(hT_d)[:, :, t0:t0 + w], h2s[:, :, 0:w], q="act")

                e_load(tiles_l[0])
                prev = None
                for n, ti in enumerate(tiles_l):
                    if n + 1 < len(tiles_l):
                        e_load(tiles_l[n + 1])
                    e_merge(ti)
                    if prev is not None:
                        e_norm(prev)
                    e_outproj(ti)
                    prev = ti
                e_norm(prev)
                P.flush()

            tokA = CT if last else 0
            stf = ExitStack()
            w2 = SB(stf, nc, "w2", [128, 22, D], BF16)
            with ExitStack() as st:
                hT = SB(st, nc, "hT2", [128, 8, T], BF16)
                w2stg = [SB(st, nc, "w2stg%d" % i, [128, 1, D], F32) for i in range(2)]
                w2q = list(range(22))

                def load_w2(nmax):
                    for _ in range(nmax):
                        if not w2q:
                            return
                        q = w2q.pop(0)
                        b = q % 2
                        P.dma([w2stg[b].tile], [], w2stg[b][:], wf2_d[l, q * 128:(q + 1) * 128, :].rearrange("(kc p) n -> p kc n", p=128))
                        P.copy(w2.tile, [w2stg[b].tile], w2[:, q:q + 1, :], w2stg[b][:], eng="pool")
                wst = [SB(st, nc, "fwst%d" % i, [128, 8, 512], F32) for i in range(2)]
                w1b = [SB(st, nc, "w1b%d" % i, [128, 8, 512], BF16) for i in range(2)]
                w3b = [SB(st, nc, "w3b%d" % i, [128, 8, 512], BF16) for i in range(2)]
                ostg = [SB(st, nc, "fost%d" % i, [128, 4, 512], BF16) for i in range(3)]
                sa = [SB(st, nc, "fsa%d" % i, [128, 512], F32) for i in range(2)]
                hTt = [Tile("hTf%d" % i) for i in range(9)]
                for ti in tiles_l:
                    t0, w = TILES[ti]
                    P.dma([hTt[ti]], [], hT[:, :, t0:t0 + w], fm(hT_d)[:, :, t0:t0 + w])
                slabs = [(c0, min(512, DFF - c0)) for c0 in range(0, DFF, 512)]

                def load_f(i):
                    c0, n = slabs[i]
                    b = i % 2
                    P.dma([wst[0].tile], [], wst[0][:, :, 0:n], wf1_d[l, :, c0:c0 + n].rearrange("(kc p) n -> p kc n", p=128))
                    P.copy(w1b[b].tile, [wst[0].tile], w1b[b][:, :, 0:n], wst[0][:, :, 0:n], eng="pool")
                    P.dma([wst[1].tile], [], wst[1][:, :, 0:n], wf3_d[l, :, c0:c0 + n].rearrange("(kc p) n -> p kc n", p=128))
                    P.copy(w3b[b].tile, [wst[1].tile], w3b[b][:, :, 0:n], wst[1][:, :, 0:n], eng="pool")
                load_f(0)
                ev = 0
                for i, (c0, n) in enumerate(slabs):
                    if i + 1 < len(slabs):
                        load_f(i + 1)
                    load_w2(4)
                    b = i % 2
                    nch = n // 128
                    for ti in tiles_l:
                        t0, w = TILES[ti]
                        og = ostg[ev % 3]
                        for j in range(nch):
                            pa = PS[(2 * ev) % 6]
                            pb = PS[(2 * ev + 1) % 6]
                            sat = sa[ev % 2]
                            ev += 1
                            P.mm(pa.tile, [w1b[b].tile, hTt[ti]], pa[:, 0:w], [(w1b[b][:, kc, j * 128:(j + 1) * 128], hT[:, kc, t0:t0 + w]) for kc in range(8)])
                            P.mm(pb.tile, [w3b[b].tile, hTt[ti]], pb[:, 0:w], [(w3b[b][:, kc, j * 128:(j + 1) * 128], hT[:, kc, t0:t0 + w]) for kc in range(8)])
                            P.act(sat.tile, [pa.tile], sat[:, 0:w], pa[:, 0:w], AF.Silu)
                            P.tt(og.tile, [sat.tile, pb.tile], og[:, j, 0:w], sat[:, 0:w], pb[:, 0:w], ALU.mult)
                        P.dma([], [og.tile], fm(ffh_d[c0:c0 + n, :])[:, :, t0:t0 + w], og[:, 0:nch, 0:w], q="act")
                load_w2(22)
                P.flush()

            with ExitStack() as st:
                hid = [SB(st, nc, "hid%d" % i, [128, 22, 512], BF16) for i in range(2)]
                xt = [SB(st, nc, "gxt%d" % i, [128, 8, 512], F32) for i in range(2)]
                if last:
                    sq = SB(st, nc, "gsq", [128, 8, 512], BF16)
                    rstd = SB(st, nc, "grstd", [128, 512], F32)
                    tmps = [SB(st, nc, "gtmp%d" % i, [128, 512], F32) for i in range(2)]
                    ho = SB(st, nc, "gho", [128, 8, 512], F32)
                    ot = SB(st, nc, "got", [128, 4, D], F32)
                for ti in tiles_l:
                    t0, w = TILES[ti]
                    b = ti % 2
                    s = 1 if ti == 0 else 0
                    P.dma([hid[b].tile], [], hid[b][:, :, 0:w], fm(ffh_d)[:, :, t0:t0 + w])
                    P.dma([xt[b].tile], [], xt[b][:, :, 0:w], fm(xT_d)[:, :, t0:t0 + w])
                    for j in range(8):
                        ps = PS[j % 4]
                        P.mm(ps.tile, [w2.tile, hid[b].tile], ps[:, 0:w], [(w2[:, kc, j * 128:(j + 1) * 128], hid[b][:, kc, 0:w]) for kc in range(22)])
                        P.stt(xt[b].tile, [ps.tile, MOD.tile, xt[b].tile], xt[b][:, j, 0:w], ps[:, 0:w], modv(l, 5, j, s), xt[b][:, j, 0:w], ALU.mult, ALU.add)
                    if not last:
                        P.dma([], [xt[b].tile], fm(xT_d)[:, :, t0:t0 + w], xt[b][:, :, 0:w], q="act")
                    else:
                        if debug:
                            P.dma([], [xt[b].tile], fm(xT_d)[:, :, t0:t0 + w], xt[b][:, :, 0:w])
                        norm_tile(xt[b], w, sq, rstd, tmps, PS[7], lambda c: ho[:, c, 0:w], ho.tile,
                                  lambda c: fng[:, c:c + 1], lambda c: zero8[:, c:c + 1])
                        for a in range(w // 128):
                            for half in range(2):
                                ps = PS[4 + (2 * a + half) % 3]
                                items = [(ps[:, cc * 128:(cc + 1) * 128], ho[:, half * 4 + cc, a * 128:(a + 1) * 128]) for cc in range(4)]
                                P.transposes(ps.tile, [ho.tile, ident.tile], items, ident[:])
                                if half == 0:
                                    P.copy(ot.tile, [ps.tile], ot[:, a, 0:512], ps[:, :], eng="dve")
                                else:
                                    P.act(ot.tile, [ps.tile], ot[:, a, 512:1024], ps[:, :], AF.Copy)
                        r0 = t0 - CT
                        P.dma([], [ot.tile], out_d[r0:r0 + w, :].rearrange("(a p) d -> p a d", p=128), ot[:, 0:w // 128, :], q="act")
                P.flush()
            stf.close()
    return nc


_PERM32 = np.concatenate([np.arange(8, 16), np.arange(0, 8), np.arange(24, 32), np.arange(16, 24)])


def _fmaj(v, n):
    v = np.asarray(v)
    sh = v.shape[:-1]
    v = v.reshape(sh + (n, 128))
    return np.ascontiguousarray(np.moveaxis(v, -1, 0))


def prep(inp):
    f = np.float32
    sh = {}
    sh["w_mod"] = np.ascontiguousarray(inp["w_mod"], f)
    sh["bmod"] = _fmaj(inp["b_mod"], 48).astype(f)
    ng = np.stack([inp["norm1_g"], inp["norm2_g"]], axis=1)
    sh["ng"] = _fmaj(ng, 8).astype(f)
    sh["fng"] = _fmaj(inp["final_norm_g"], 8).astype(f)
    sh["w_in"] = np.ascontiguousarray(inp["w_in"], f)
    sh["w_krp"] = np.ascontiguousarray(inp["w_in"][:, :, C_KR:C_KR + 32][:, :, _PERM32], f)
    sh["convw"] = _fmaj(inp["conv_a_w"], 4).astype(f)
    sh["lcw"] = _fmaj(inp["lru_conv_w"], 4).astype(f)
    sh["lcb"] = _fmaj(inp["lru_conv_b"], 4).astype(f)
    wa = np.stack([inp["lru_w_a"], inp["lru_w_x"]], axis=2)
    bd = np.zeros((128, L, 2, 2, 4, 128), f)
    for c in range(4):
        for hh in range(2):
            blk = wa[:, :, :, 2 * c + hh]
            bd[hh * 64:(hh + 1) * 64, :, :, :, c, hh * 64:(hh + 1) * 64] = np.moveaxis(blk, 3, 0)
    sh["lwa"] = bd
    lbs = np.stack([inp["lru_b_a"], inp["lru_b_x"], inp["lru_lam"]], axis=1)
    sh["lb"] = _fmaj(lbs, 4).astype(f)
    cl = np.stack([inp["cmlp_ln_g"], inp["cmlp_ln_b"]], axis=1)
    sh["clg"] = _fmaj(cl, 4).astype(f)
    sh["cws"] = np.ascontiguousarray(np.transpose(inp["cmlp_w_s"], (3, 0, 1, 2)), f)
    sh["cbs"] = np.ascontiguousarray(np.broadcast_to(inp["cmlp_b_s"][None], (128, L, 4, 128)), f)
    sh["qng"] = _fmaj(inp["mla_q_norm_g"], 3).astype(f)
    sh["kvng"] = _fmaj(inp["mla_kv_norm_g"], 2).astype(f)
    wq = np.asarray(inp["mla_w_q_up"], f)
    sh["wq"] = np.ascontiguousarray(wq)
    wqp = wq.reshape(L, 384, 8, 96).copy()
    wqp[:, :, :, 64:96] = wqp[:, :, :, 64:96][..., _PERM32]
    sh["wqp"] = np.ascontiguousarray(wqp.reshape(L, 384, 768))
    wkv = np.asarray(inp["mla_w_kv_up"], f).reshape(L, 256, 8, 128)
    sh["wk"] = np.ascontiguousarray(wkv[..., :64].reshape(L, 256, 512))
    sh["wv"] = np.ascontiguousarray(wkv[..., 64:].reshape(L, 256, 512))
    for k in ("w_branch", "w_out", "w_ff1", "w_ff3", "w_ff2"):
        sh[k] = np.ascontiguousarray(inp[k], f)
    t = np.arange(S)
    inv = (np.float32(10000.0) ** (-np.arange(8, dtype=f) / np.float32(8))).astype(f)
    ar = (t // 64).astype(f)[:, None] * inv
    ac = (t % 64).astype(f)[:, None] * inv
    cos32 = np.concatenate([np.cos(ar), np.cos(ar), np.cos(ac), np.cos(ac)], axis=1).astype(f)
    sin32 = np.concatenate([-np.sin(ar), np.sin(ar), -np.sin(ac), np.sin(ac)], axis=1).astype(f)
    rope = np.zeros((128, 2, S), f)
    for base in (0, 64):
        rope[base:base + 32, 0] = cos32.T
        rope[base:base + 32, 1] = sin32.T
    sh["rope"] = rope
    sh["ident"] = np.eye(128, dtype=f)
    maps = []
    cc = _fmaj(inp["c_ctx"], 8).astype(f)
    for b in range(8):
        m = dict(sh)
        m["x"] = np.ascontiguousarray(inp["x"][b], f)
        m["ctx"] = np.ascontiguousarray(inp["ctx"][b], f)
        m["cvec"] = np.ascontiguousarray(np.stack([_fmaj(inp["c"][b], 8), cc], axis=-1), f)
        maps.append(m)
    return maps


def kernel(**inputs):
    inputs = {k: np.asarray(v) for k, v in inputs.items()}
    maps = prep(inputs)
    nc = build()
    res = run_bass_kernel_spmd(nc, maps, core_ids=list(range(8)))
    return np.stack([np.asarray(r["out"], np.float32) for r in res.results], axis=0)
```

```python
import numpy as np
from contextlib import ExitStack
import concourse.bass as bass
import concourse.mybir as mybir
from concourse.bass_utils import run_bass_kernel_spmd

F32 = mybir.dt.float32
BF16 = mybir.dt.bfloat16
AF = mybir.ActivationFunctionType
ALU = mybir.AluOpType

D = 1024
S = 4096
CT = 256
T = S + CT
L = 4
INW = 8352
DFF = 2816
EPS = 1e-6
SCALE = 96.0 ** -0.5
TILES = [(0, 256)] + [(256 + 512 * i, 512) for i in range(8)]
NDMA_SEM = 24

C_LRUX, C_KV, C_KR, C_LRUG, C_QL, C_CB, C_CC, C_CX, C_CU, C_CV, C_GATE = (
    0, 512, 768, 800, 1312, 1696, 2208, 2720, 3232, 3744, 4256)


class Tile:
    __slots__ = ("name", "w", "readers", "psum")

    def __init__(self, name="", psum=False):
        self.name = name
        self.w = None
        self.readers = []
        self.psum = psum


class Prog:
    ENG = ("pe", "act", "dve", "pool", "sp")
    CE = ("pe", "act", "dve", "pool")

    def __init__(self, nc, stack):
        self.nc = nc
        self.sem = {e: stack.enter_context(nc.semaphore("s_" + e)) for e in self.CE}
        self.dsem = [stack.enter_context(nc.semaphore("d%d" % i)) for i in range(NDMA_SEM)]
        self.cnt = {e: 0 for e in self.CE}
        self.ndma = 0
        self.known = {e: {f: 0 for f in self.CE} for e in self.ENG}
        self.kdma = {e: set() for e in self.ENG}
        self.ops = {e: [] for e in self.ENG}

    def _deps(self, eng, reads, writes):
        deps = []
        for t in reads:
            if t.w is not None:
                deps.append(t.w)
            if t.psum:
                deps.extend(r for r in t.readers if r[0] != eng)
        for t in writes:
            if t.w is not None and t.w[0] != eng:
                deps.append(t.w)
            deps.extend(r for r in t.readers if r[0] != eng)
        best = {}
        kn = self.known[eng]
        for (e, idx) in deps:
            if e == "dma":
                if idx in self.kdma[eng]:
                    continue
                self.kdma[eng].add(idx)
                best[("d", idx)] = (self.dsem[idx % NDMA_SEM], 16 * (idx // NDMA_SEM + 1))
            else:
                if idx <= kn[e]:
                    continue
                kn[e] = idx
                best[e] = (self.sem[e], idx)
        return list(best.values())

    def _mark(self, me, reads, writes):
        for t in reads:
            t.readers.append(me)
        for t in writes:
            t.w = me
            t.readers = []

    def op(self, eng, writes, reads, fn):
        waits = self._deps(eng, reads, writes)
        if eng == "pe":
            waits = [(s, v) for (s, v) in waits if s is not self.sem["pe"]]
        self.cnt[eng] += 1
        self.ops[eng].append((waits, fn, (self.sem[eng], 1)))
        self._mark((eng, self.cnt[eng]), reads, writes)

    def dma(self, writes, reads, out_ap, in_ap, q="sp"):
        k = self.ndma
        self.ndma += 1
        waits = self._deps(q, reads, writes)
        if k >= NDMA_SEM:
            kk = k - NDMA_SEM
            if kk not in self.kdma[q]:
                self.kdma[q].add(kk)
                waits.append((self.dsem[kk % NDMA_SEM], 16 * (kk // NDMA_SEM + 1)))

        def fn(e, out_ap=out_ap, in_ap=in_ap):
            return e.dma_start(out=out_ap, in_=in_ap)
        self.ops[q].append((waits, fn, (self.dsem[k % NDMA_SEM], 16)))
        self._mark(("dma", k), reads, writes)

    def flush(self):
        nc = self.nc
        tail = []
        for k in range(max(0, self.ndma - NDMA_SEM), self.ndma):
            tail.append((self.dsem[k % NDMA_SEM], 16 * (k // NDMA_SEM + 1)))
        finals = [(self.sem[e], self.cnt[e]) for e in self.CE if self.cnt[e] > 0]
        ops = self.ops

        def replay(handle, lst, extra):
            for waits, fn, inc in lst:
                for s, v in waits:
                    handle.wait_ge(s, v)
                fn(handle).then_inc(inc[0], inc[1])
            for s, v in extra:
                handle.wait_ge(s, v)

        with nc.Block() as block:
            @block.tensor
            def _(e):
                replay(e, ops["pe"], finals)

            @block.scalar
            def _(e):
                replay(e, ops["act"], finals)

            @block.vector
            def _(e):
                replay(e, ops["dve"], finals)

            @block.gpsimd
            def _(e):
                replay(e, ops["pool"], finals)

            @block.sync
            def _(e):
                replay(e, ops["sp"], finals + tail)
        self.ops = {e: [] for e in self.ENG}
        for e in self.ENG:
            for f in self.CE:
                self.known[e][f] = self.cnt[f]
            self.kdma[e] = set(range(max(0, self.ndma - 2 * NDMA_SEM), self.ndma))

    def mm(self, out_t, reads, out_ap, pairs):
        n = len(pairs)

        def fn(e):
            ins = None
            for i, (l, r) in enumerate(pairs):
                ins = e.matmul(out_ap, l, r, start=(i == 0), stop=(i == n - 1))
            return ins
        self.op("pe", [out_t], reads, fn)

    def mm1(self, out_t, reads, out_ap, l, r, start, stop):
        self.op("pe", [out_t], reads, lambda e: e.matmul(out_ap, l, r, start=start, stop=stop))

    def transposes(self, out_t, reads, items, ident_ap):
        def fn(e):
            ins = None
            for (o, i) in items:
                ins = e.transpose(o, i, ident_ap)
            return ins
        self.op("pe", [out_t], reads, fn)

    def act(self, out_t, reads, out_ap, in_ap, func, scale=1.0, bias=0.0):
        self.op("act", [out_t], reads,
                lambda e: e.activation(out=out_ap, in_=in_ap, func=func, scale=scale, bias=bias))

    def tt(self, out_t, reads, out_ap, a_ap, b_ap, op, eng="dve"):
        self.op(eng, [out_t], reads, lambda e: e.tensor_tensor(out=out_ap, in0=a_ap, in1=b_ap, op=op))

    def ts(self, out_t, reads, out_ap, in_ap, s1, op0, s2=None, op1=None, eng="dve"):
        if op1 is None:
            self.op(eng, [out_t], reads,
                    lambda e: e.tensor_scalar(out=out_ap, in0=in_ap, scalar1=s1, scalar2=None, op0=op0))
        else:
            self.op(eng, [out_t], reads,
                    lambda e: e.tensor_scalar(out=out_ap, in0=in_ap, scalar1=s1, scalar2=s2, op0=op0, op1=op1))

    def stt(self, out_t, reads, out_ap, in0, scalar, in1, op0, op1, eng="dve"):
        self.op(eng, [out_t], reads,
                lambda e: e.scalar_tensor_tensor(out=out_ap, in0=in0, scalar=scalar, in1=in1, op0=op0, op1=op1))

    def copy(self, out_t, reads, out_ap, in_ap, eng="dve"):
        self.op(eng, [out_t], reads, lambda e: e.tensor_copy(out=out_ap, in_=in_ap))

    def memset(self, out_t, out_ap, val, eng="pool"):
        self.op(eng, [out_t], [], lambda e: e.memset(out_ap, val))


class SB:
    _n = [0]

    def __init__(self, st, nc, name, shape, dtype, psum=False):
        SB._n[0] += 1
        name = "sb%d_%s" % (SB._n[0], name)
        if psum:
            self.t = st.enter_context(nc.psum_tensor(name, shape, dtype))
        else:
            self.t = st.enter_context(nc.sbuf_tensor(name, shape, dtype))
        self.tile = Tile(name, psum=psum)

    def __getitem__(self, k):
        return self.t[k]


def build(nlayers=L, debug=False):
    nc = bass.Bass("TRN2", target_bir_lowering=False)
    okind = "ExternalOutput" if debug else None

    def din(name, shape, dt=F32):
        return nc.dram_tensor(name, list(shape), dt, kind="ExternalInput").ap()

    def dscr(name, shape, dt):
        if okind:
            return nc.dram_tensor(name, list(shape), dt, kind=okind).ap()
        return nc.dram_tensor(name, list(shape), dt).ap()

    x_d = din("x", [S, D]); ctx_d = din("ctx", [CT, D]); cvec_d = din("cvec", [128, 8, 2])
    wmod_d = din("w_mod", [L, D, 6 * D]); bmod_d = din("bmod", [128, L, 48])
    ng_d = din("ng", [128, L, 2, 8]); fng_d = din("fng", [128, 8])
    win_d = din("w_in", [L, D, INW]); wkrp_d = din("w_krp", [L, D, 32])
    convw_d = din("convw", [128, L, 3, 4]); lcw_d = din("lcw", [128, L, 4, 4]); lcb_d = din("lcb", [128, L, 4])
    lwa_d = din("lwa", [128, L, 2, 2, 4, 128]); lb_d = din("lb", [128, L, 3, 2, 4])
    clg_d = din("clg", [128, L, 2, 4]); cws_d = din("cws", [128, L, 4, 128]); cbs_d = din("cbs", [128, L, 4, 128])
    qng_d = din("qng", [128, L, 3]); kvng_d = din("kvng", [128, L, 2])
    wq_d = din("wq", [L, 384, 768]); wqp_d = din("wqp", [L, 384, 768])
    wk_d = din("wk", [L, 256, 512]); wv_d = din("wv", [L, 256, 512])
    wbr_d = din("w_branch", [L, 4, 512, D]); wout_d = din("w_out", [L, D, D])
    wf1_d = din("w_ff1", [L, D, DFF]); wf3_d = din("w_ff3", [L, D, DFF]); wf2_d = din("w_ff2", [L, DFF, D])
    rope_d = din("rope", [128, 2, S]); ident_d = din("ident", [128, 128])
    out_d = nc.dram_tensor("out", [S, D], F32, kind="ExternalOutput").ap()

    xT_d = dscr("xT", [D, T], F32)
    hT_d = dscr("hT", [D, T], BF16)
    lrux_d = dscr("p_lrux", [512, T], BF16); kvl_d = dscr("p_kvl", [256, T], BF16)
    lrug_d = dscr("p_lrug", [512, T], BF16); ql_d = dscr("p_ql", [384, T], BF16)
    cb_d = dscr("p_cb", [512, T], BF16); cc_d = dscr("p_cc", [512, T], BF16); cx_d = dscr("p_cx", [512, T], BF16)
    cu_d = dscr("p_cu", [512, T], BF16); gate_d = dscr("p_gate", [4096, T], BF16)
    kr_d = dscr("p_kr", [32, T], BF16)
    yall_d = dscr("yall", [2048, T], BF16)
    ffh_d = dscr("ffh", [DFF, T], BF16)

    def fm(ap, p=128):
        return ap.rearrange("(c p) t -> p c t", p=p)

    with ExitStack() as gst:
        P = Prog(nc, gst)
        PS = [SB(gst, nc, "ps%d" % i, [128, 512], F32, psum=True) for i in range(8)]
        ident = SB(gst, nc, "ident", [128, 128], F32)
        ones_bf = SB(gst, nc, "ones_bf", [128, 128], BF16)
        ones_f = SB(gst, nc, "ones_f", [128, 64], F32)
        MOD = SB(gst, nc, "MOD", [128, L, 48, 2], F32)
        AM = SB(gst, nc, "AM", [128, L, 2, 8, 2], F32)
        ngt = SB(gst, nc, "ngt", [128, L, 2, 8], F32)
        fng = SB(gst, nc, "fng", [128, 8], F32)
        convw = SB(gst, nc, "convw", [128, L, 3, 4], F32)
        lcw = SB(gst, nc, "lcw", [128, L, 4, 4], F32)
        lcb = SB(gst, nc, "lcb", [128, L, 4], F32)
        lb = SB(gst, nc, "lb", [128, L, 3, 2, 4], F32)
        lsc = SB(gst, nc, "lsc", [128, L, 2, 2, 4], F32)
        clg = SB(gst, nc, "clg", [128, L, 2, 4], F32)
        qng = SB(gst, nc, "qng", [128, L, 3], F32)
        kvng = SB(gst, nc, "kvng", [128, L, 2], F32)
        zero8 = SB(gst, nc, "zero8", [128, 8], F32)

        consts = [ident, ones_bf, ones_f, MOD, AM, ngt, fng, convw, lcw, lcb, lb, lsc, clg, qng, kvng, zero8]

        P.dma([ident.tile], [], ident[:], ident_d[:])
        P.memset(ones_bf.tile, ones_bf[:], 1.0)
        P.memset(ones_f.tile, ones_f[:], 1.0)
        P.memset(zero8.tile, zero8[:], 0.0)
        for (sbt, dd) in ((ngt, ng_d), (fng, fng_d), (convw, convw_d), (lcw, lcw_d), (lcb, lcb_d), (lb, lb_d),
                          (clg, clg_d), (qng, qng_d), (kvng, kvng_d)):
            P.dma([sbt.tile], [], sbt[:], dd[:])
        lam_ap = lb[:, :, 2, :, :]
        P.act(lsc.tile, [lb.tile], lsc[:, :, 0, :, :], lam_ap, AF.Exp, scale=-1.0)
        P.act(lsc.tile, [lsc.tile], lsc[:, :, 0, :, :], lsc[:, :, 0, :, :], AF.Ln, scale=1.0, bias=1.0)
        P.ts(lsc.tile, [lsc.tile], lsc[:, :, 1, :, :], lsc[:, :, 0, :, :], -16.0, ALU.mult)
        P.ts(lsc.tile, [lsc.tile], lsc[:, :, 0, :, :], lsc[:, :, 0, :, :], -8.0, ALU.mult)
        P.flush()

        with ExitStack() as st:
            xin = [SB(st, nc, "xin%d" % i, [128, 4, D], F32) for i in range(2)]
            xts = [SB(st, nc, "xts%d" % i, [128, 8, 512], F32) for i in range(2)]
            for ti, (t0, w) in enumerate(TILES):
                b = ti % 2
                nsub = w // 128
                src = ctx_d if ti == 0 else x_d
                r0 = 0 if ti == 0 else t0 - CT
                P.dma([xin[b].tile], [], xin[b][:, 0:nsub, :], src[r0:r0 + w, :].rearrange("(a p) d -> p a d", p=128))
                for c in range(8):
                    ps = PS[c]
                    items = [(ps[:, a * 128:(a + 1) * 128], xin[b][:, a, c * 128:(c + 1) * 128]) for a in range(nsub)]
                    P.transposes(ps.tile, [xin[b].tile, ident.tile], items, ident[:])
                    if c % 2 == 0:
                        P.copy(xts[b].tile, [ps.tile], xts[b][:, c, 0:w], ps[:, 0:w], eng="dve")
                    else:
                        P.act(xts[b].tile, [ps.tile], xts[b][:, c, 0:w], ps[:, 0:w], AF.Copy)
                P.dma([], [xts[b].tile], fm(xT_d)[:, :, t0:t0 + w], xts[b][:, :, 0:w])
            P.flush()

        with ExitStack() as st:
            cv = SB(st, nc, "cv", [128, 8, 2], F32)
            sv = SB(st, nc, "sv", [128, 8, 2], F32)
            bm = SB(st, nc, "bm", [128, L, 48], F32)
            wst = [SB(st, nc, "wmst%d" % i, [128, 8, 768], F32) for i in range(2)]
            wmb = [SB(st, nc, "wmb%d" % i, [128, 8, 768], BF16) for i in range(2)]
            svb = SB(st, nc, "svb", [128, 8, 2], BF16)
            P.dma([cv.tile], [], cv[:], cvec_d[:])
            P.dma([bm.tile], [], bm[:], bmod_d[:])
            P.act(sv.tile, [cv.tile], sv[:], cv[:], AF.Silu)
            P.copy(svb.tile, [sv.tile], svb[:], sv[:], eng="dve")
            k = 0
            for l in range(nlayers):
                for sl in range(8):
                    b = k % 2
                    P.dma([wst[b].tile], [], wst[b][:], wmod_d[l, :, sl * 768:(sl + 1) * 768].rearrange("(kc p) n -> p kc n", p=128))
                    if k % 3 == 0:
                        P.copy(wmb[b].tile, [wst[b].tile], wmb[b][:], wst[b][:], eng="dve")
                    elif k % 3 == 1:
                        P.copy(wmb[b].tile, [wst[b].tile], wmb[b][:], wst[b][:], eng="pool")
                    else:
                        P.act(wmb[b].tile, [wst[b].tile], wmb[b][:], wst[b][:], AF.Copy)
                    ps = PS[k % 4]
                    for jj in range(6):
                        P.mm(ps.tile, [wmb[b].tile, svb.tile], ps[:, jj * 2:jj * 2 + 2],
                             [(wmb[b][:, kc, jj * 128:(jj + 1) * 128], svb[:, kc, :]) for kc in range(8)])
                    for s in range(2):
                        P.tt(MOD.tile, [ps.tile, bm.tile], MOD[:, l, sl * 6:sl * 6 + 6, s],
                             ps[:, 0:12].rearrange("p (j s) -> p j s", s=2)[:, :, s], bm[:, l, sl * 6:sl * 6 + 6], ALU.add)
                    k += 1
                for which in range(2):
                    for s in range(2):
                        sc = MOD[:, l, (8 + 24 * which):(16 + 24 * which), s]
                        P.stt(AM.tile, [MOD.tile, ngt.tile], AM[:, l, which, :, s], sc, 1.0, ngt[:, l, which, :], ALU.add, ALU.mult)
            P.flush()

        def modv(l, m, c, s):
            return MOD[:, l, m * 8 + c, s:s + 1]

        def norm_tile(x_sb, w, sq, rstd, tmps, psn, out_fn, out_tiles, a_fn, b_fn, sq_pool=False):
            if sq_pool:
                P.tt(sq.tile, [x_sb.tile], sq[:, :, 0:w], x_sb[:, :, 0:w], x_sb[:, :, 0:w], ALU.mult, eng="pool")
            else:
                P.act(sq.tile, [x_sb.tile], sq[:, :, 0:w], x_sb[:, :, 0:w], AF.Square)
            P.mm(psn.tile, [sq.tile, ones_bf.tile], psn[:, 0:w], [(ones_bf[:], sq[:, c, 0:w]) for c in range(8)])
            P.act(rstd.tile, [psn.tile], rstd[:, 0:w], psn[:, 0:w], AF.Sqrt, scale=1.0 / D, bias=EPS)
            P.op("dve", [rstd.tile], [rstd.tile], lambda e: e.reciprocal(out=rstd[:, 0:w], in_=rstd[:, 0:w]))
            for c in range(8):
                tm = tmps[c % 2]
                P.tt(tm.tile, [x_sb.tile, rstd.tile], tm[:, 0:w], x_sb[:, c, 0:w], rstd[:, 0:w], ALU.mult)
                P.act(out_tiles[c] if isinstance(out_tiles, list) else out_tiles, [tm.tile, MOD.tile, AM.tile],
                      out_fn(c), tm[:, 0:w], AF.Identity, scale=a_fn(c), bias=b_fn(c))

        dbg = {}
        for l in range(nlayers):
            last = (l == L - 1)
            tiles_l = [ti for ti in range(9) if not (last and ti == 0)]

            with ExitStack() as st1:
                hT = SB(st1, nc, "hT", [128, 8, T], BF16)
                hTt = [Tile("hT%d" % i) for i in range(9)]
                with ExitStack() as st:
                    xt = [SB(st, nc, "xt%d" % i, [128, 8, 512], F32) for i in range(2)]
                    sqs = [SB(st, nc, "sq%d" % i, [128, 8, 512], BF16) for i in range(2)]
                    rstds = [SB(st, nc, "rstd%d" % i, [128, 512], F32) for i in range(2)]
                    tmps = [SB(st, nc, "ntmp%d" % i, [128, 512], F32) for i in range(4)]
                    wst = [SB(st, nc, "wst%d" % i, [128, 8, 512], F32) for i in range(2)]
                    wbf = [SB(st, nc, "wbf%d" % i, [128, 8, 512], BF16) for i in range(3)]
                    ostg = [SB(st, nc, "ostg%d" % i, [128, 4, 512], BF16) for i in range(3)]
                    def n1_load(ti):
                        t0, w = TILES[ti]
                        b = ti % 2
                        P.dma([xt[b].tile], [], xt[b][:, :, 0:w], fm(xT_d)[:, :, t0:t0 + w])

                    def n1_norm(ti):
                        t0, w = TILES[ti]
                        b = ti % 2
                        s = 1 if ti == 0 else 0
                        norm_tile(xt[b], w, sqs[b], rstds[b], tmps[2 * b:2 * b + 2], PS[6 + b],
                                  lambda c: hT[:, c, t0:t0 + w], hTt[ti],
                                  lambda c: AM[:, l, 0, c, s:s + 1], lambda c: modv(l, 0, c, s), sq_pool=True)
                    n1_load(0)
                    n1_load(1)
                    n1_norm(0)
                    regions = [(C_LRUX, 512, None, lrux_d, False), (C_KV, 256, None, kvl_d, False),
                               (C_LRUG, 512, AF.Gelu_apprx_tanh, lrug_d, True), (C_QL, 384, None, ql_d, True),
                               (C_CB, 512, None, cb_d, True), (C_CC, 512, None, cc_d, True), (C_CX, 512, None, cx_d, True),
                               (C_CU, 512, AF.Gelu_apprx_tanh, cu_d, True)]
                    for g in range(8):
                        regions.append((C_GATE + 512 * g, 512, AF.Sigmoid, gate_d[512 * g:512 * (g + 1), :], True))
                    slabs = regions

                    def load_slab(i):
                        c0, n, fn_, dst, skipc = slabs[i]
                        b = i % 2
                        b3 = i % 3
                        P.dma([wst[b].tile], [], wst[b][:, :, 0:n], win_d[l, :, c0:c0 + n].rearrange("(kc p) n -> p kc n", p=128))
                        P.copy(wbf[b3].tile, [wst[b].tile], wbf[b3][:, :, 0:n], wst[b][:, :, 0:n], eng="pool")
                    load_slab(0)
                    load_slab(1)
                    load_slab(2)
                    evc = [0]

                    def slab_tile(i, ti):
                        c0, n, fn_, dst, skipc = slabs[i]
                        b = i % 3
                        nch = n // 128
                        t0, w = TILES[ti]
                        if last and ti == 0 and skipc:
                            return
                        og = ostg[evc[0] % 3]
                        for j in range(nch):
                            ps = PS[evc[0] % 6]
                            evc[0] += 1
                            P.mm(ps.tile, [wbf[b].tile, hTt[ti]], ps[:, 0:w],
                                 [(wbf[b][:, kc, j * 128:(j + 1) * 128], hT[:, kc, t0:t0 + w]) for kc in range(8)])
                            if fn_ is not None:
                                P.act(og.tile, [ps.tile], og[:, j, 0:w], ps[:, 0:w], fn_)
                            elif evc[0] % 2 == 0:
                                P.act(og.tile, [ps.tile], og[:, j, 0:w], ps[:, 0:w], AF.Copy)
                            else:
                                P.copy(og.tile, [ps.tile], og[:, j, 0:w], ps[:, 0:w], eng="dve")
                        P.dma([], [og.tile], fm(dst)[:, :, t0:t0 + w], og[:, 0:nch, 0:w], q="act")
                    for ti in range(9):
                        if ti + 1 < 9:
                            if ti + 2 < 9:
                                n1_load(ti + 2)
                            n1_norm(ti + 1)
                        slab_tile(0, ti)
                        slab_tile(1, ti)
                        slab_tile(2, ti)
                    load_slab(3)
                    for i in range(3, len(slabs)):
                        if i + 1 < len(slabs):
                            load_slab(i + 1)
                        for ti in range(9):
                            slab_tile(i, ti)
                    P.flush()
                with ExitStack() as st:
                    rope = SB(st, nc, "rope", [128, 2, S], F32)
                    P.dma([rope.tile], [], rope[:], rope_d[:])
                    wkr_st = SB(st, nc, "wkr_st", [128, 8, 64], F32)
                    wkr = SB(st, nc, "wkr", [128, 8, 64], BF16)
                    krt = [SB(st, nc, "krt%d" % i, [32, 512], F32) for i in range(2)]
                    kro = [SB(st, nc, "kro%d" % i, [32, 512], BF16) for i in range(2)]
                    P.dma([wkr_st.tile], [], wkr_st[:, :, 0:32], win_d[l, :, C_KR:C_KR + 32].rearrange("(kc p) n -> p kc n", p=128))
                    P.dma([wkr_st.tile], [], wkr_st[:, :, 32:64], wkrp_d[l].rearrange("(kc p) n -> p kc n", p=128))
                    P.copy(wkr.tile, [wkr_st.tile], wkr[:], wkr_st[:], eng="pool")
                    for ti, (t0, w) in enumerate(TILES):
                        pa, pb = PS[0], PS[1]
                        P.mm(pa.tile, [wkr.tile, hT.tile], pa[0:32, 0:w], [(wkr[:, kc, 0:32], hT[:, kc, t0:t0 + w]) for kc in range(8)])
                        ko = kro[ti % 2]
                        if ti == 0:
                            P.copy(ko.tile, [pa.tile], ko[:, 0:w], pa[0:32, 0:w], eng="dve")
                        else:
                            P.mm(pb.tile, [wkr.tile, hT.tile], pb[0:32, 0:w], [(wkr[:, kc, 32:64], hT[:, kc, t0:t0 + w]) for kc in range(8)])
                            s0 = t0 - CT
                            P.tt(krt[0].tile, [pa.tile, rope.tile], krt[0][:, 0:w], pa[0:32, 0:w], rope[0:32, 0, s0:s0 + w], ALU.mult)
                            P.tt(krt[1].tile, [pb.tile, rope.tile], krt[1][:, 0:w], pb[0:32, 0:w], rope[0:32, 1, s0:s0 + w], ALU.mult)
                            P.tt(ko.tile, [krt[0].tile, krt[1].tile], ko[:, 0:w], krt[0][:, 0:w], krt[1][:, 0:w], ALU.add, eng="pool")
                        P.dma([], [ko.tile], kr_d[:, t0:t0 + w], ko[:, 0:w])
                    P.flush()

                with ExitStack() as st:
                    wst = SB(st, nc, "cwst", [128, 8, 512], F32)
                    wv = SB(st, nc, "cwv", [128, 8, 512], BF16)
                    ws_f = SB(st, nc, "ws_f", [128, 4, 128], F32)
                    ws_b = SB(st, nc, "ws_b", [128, 4, 128], BF16)
                    bs_bc = SB(st, nc, "bs_bc", [128, 4, 128], F32)
                    Rg = SB(st, nc, "Rg", [128, 4, 128], F32)
                    vg = [SB(st, nc, "vg%d" % i, [128, 4, 512], F32) for i in range(2)]
                    vn = [SB(st, nc, "vn%d" % i, [128, 4, 512], BF16) for i in range(2)]
                    stt_ = [SB(st, nc, "bst%d" % i, [128, 4, 8], F32) for i in range(2)]
                    rs4 = [SB(st, nc, "rs4%d" % i, [128, 8], F32) for i in range(2)]
                    ut = [SB(st, nc, "ut%d" % i, [128, 4, 512], BF16) for i in range(2)]
                    yc = [SB(st, nc, "yc%d" % i, [128, 4, 512], BF16) for i in range(2)]
                    tq = [SB(st, nc, "tq%d" % i, [128, 4, 128], F32) for i in range(2)]
                    P.dma([wst.tile], [], wst[:], win_d[l, :, C_CV:C_CV + 512].rearrange("(kc p) n -> p kc n", p=128))
                    P.copy(wv.tile, [wst.tile], wv[:], wst[:], eng="pool")
                    P.dma([ws_f.tile], [], ws_f[:], cws_d[:, l, :, :])
                    P.dma([bs_bc.tile], [], bs_bc[:], cbs_d[:, l, :, :])
                    P.copy(ws_b.tile, [ws_f.tile], ws_b[:], ws_f[:], eng="dve")
                    for g in range(4):
                        P.mm(PS[6].tile, [ones_bf.tile, ws_b.tile], PS[6][:, g * 128:(g + 1) * 128], [(ones_bf[:], ws_b[:, g, :])])
                        P.stt(Rg.tile, [PS[6].tile, clg.tile, bs_bc.tile], Rg[:, g, :], PS[6][:, g * 128:(g + 1) * 128],
                              clg[:, l, 1, g:g + 1], bs_bc[:, g, :], ALU.mult, ALU.add)
                    tp5 = {ti: n for n, ti in enumerate(tiles_l)}

                    def cA(ti):
                        t0, w = TILES[ti]
                        n = tp5[ti]
                        b = n % 2
                        na = w // 128
                        P.dma([ut[b].tile], [], ut[b][:, :, 0:w], fm(cu_d)[:, :, t0:t0 + w])
                        for a in range(na):
                            q0 = t0 + a * 128
                            pv = PS[a]
                            P.mm(pv.tile, [hT.tile, wv.tile], pv[:, :], [(hT[:, kc, q0:q0 + 128], wv[:, kc, :]) for kc in range(8)])
                            P.act(vg[b].tile, [pv.tile], vg[b][:, a, :], pv[:, :], AF.Gelu_apprx_tanh)
                            P.op("dve", [stt_[b].tile], [vg[b].tile], lambda e, b=b, a=a: e.bn_stats(out=stt_[b][:, a, 0:6], in_=vg[b][:, a, :]))
                            P.op("dve", [stt_[b].tile], [stt_[b].tile], lambda e, b=b, a=a: e.bn_aggr(out=stt_[b][:, a, 6:8], in_=stt_[b][:, a, 0:6]))
                        P.act(rs4[b].tile, [stt_[b].tile], rs4[b][:, 0:na], stt_[b][:, 0:na, 7], AF.Sqrt, scale=1.0, bias=EPS)
                        P.op("dve", [rs4[b].tile], [rs4[b].tile], lambda e, b=b, na=na: e.reciprocal(out=rs4[b][:, 4:4 + na], in_=rs4[b][:, 0:na]))

                    def cB(ti):
                        t0, w = TILES[ti]
                        n = tp5[ti]
                        b = n % 2
                        na = w // 128
                        for a in range(na):
                            P.ts(vn[b].tile, [vg[b].tile, stt_[b].tile, rs4[b].tile], vn[b][:, a, :], vg[b][:, a, :], stt_[b][:, a, 6:7], ALU.subtract,
                                 rs4[b][:, 4 + a:5 + a], ALU.mult)
                        for a in range(na):
                            pm = PS[4 + a % 2]
                            tqa = tq[a % 2]
                            for g in range(4):
                                P.mm(pm.tile, [vn[b].tile, ws_b.tile], pm[:, g * 128:(g + 1) * 128], [(vn[b][:, a, g * 128:(g + 1) * 128], ws_b[:, g, :])])
                            for g in range(4):
                                P.stt(tqa.tile, [pm.tile, clg.tile, Rg.tile], tqa[:, g, :], pm[:, g * 128:(g + 1) * 128],
                                      clg[:, l, 0, g:g + 1], Rg[:, g, :], ALU.mult, ALU.add)
                            P.tt(yc[b].tile, [tqa.tile, ut[b].tile], yc[b][:, :, a * 128:(a + 1) * 128], tqa[:],
                                 ut[b][:, :, a * 128:(a + 1) * 128], ALU.mult, eng="pool")
                        P.dma([], [yc[b].tile], fm(yall_d[1024:1536, :])[:, :, t0:t0 + w], yc[b][:, :, 0:w], q="act")
                    cA(tiles_l[0])
                    for n, ti in enumerate(tiles_l):
                        if n + 1 < len(tiles_l):
                            cA(tiles_l[n + 1])
                        cB(ti)
                    P.flush()

            with ExitStack() as st:
                Bt = [SB(st, nc, "Bt%d" % i, [128, T], BF16) for i in range(2)]
                Ct = [SB(st, nc, "Ct%d" % i, [128, T], BF16) for i in range(2)]
                Xt = [SB(st, nc, "Xt%d" % i, [128, T], BF16) for i in range(2)]
                cxb = [SB(st, nc, "cxb%d" % i, [128, T + 4], BF16) for i in range(2)]
                ya = [SB(st, nc, "ya%d" % i, [128, T], BF16) for i in range(2)]
                dwa = [SB(st, nc, "dwa%d" % i, [128, 3, 128], BF16) for i in range(2)]
                a0 = 0 if not last else CT
                cxt = [[Tile("cxt%d_%d" % (i, j)) for j in range(3)] for i in range(2)]
                for i in range(2):
                    P.memset(cxb[i].tile, cxb[i][:], 0.0)

                def ca_load(c):
                    b = c % 2
                    rows = slice(c * 128, (c + 1) * 128)
                    P.dma([Bt[b].tile], [], Bt[b][:, a0:T], cb_d[rows, a0:T])
                    P.dma([Ct[b].tile], [], Ct[b][:, a0:T], cc_d[rows, a0:T])
                    P.dma([Xt[b].tile], [], Xt[b][:, a0:T], cx_d[rows, a0:T])
                ca_load(0)
                ev = 0
                for c in range(4):
                    b = c % 2
                    if c + 1 < 4:
                        ca_load(c + 1)
                    for k in range(3):
                        P.ts(dwa[b].tile, [ident.tile, convw.tile], dwa[b][:, k, :], ident[:], convw[:, l, k, c:c + 1], ALU.mult)
                    if not last:
                        P.tt(cxt[b][0], [cxb[b].tile, Ct[b].tile, Xt[b].tile], cxb[b][:, 1:257], Ct[b][:, 0:256], Xt[b][:, 0:256], ALU.mult, eng="pool")
                    P.tt(cxt[b][1], [cxb[b].tile, Ct[b].tile, Xt[b].tile], cxb[b][:, 259:259 + 2048], Ct[b][:, 256:256 + 2048], Xt[b][:, 256:256 + 2048], ALU.mult)
                    P.tt(cxt[b][2], [cxb[b].tile, Ct[b].tile, Xt[b].tile], cxb[b][:, 259 + 2048:259 + 4096], Ct[b][:, 256 + 2048:T], Xt[b][:, 256 + 2048:T], ALU.mult, eng="pool")
                    for ti in tiles_l:
                        t0, w = TILES[ti]
                        p0 = t0 + (1 if ti == 0 else 3)
                        ps = PS[ev % 6]
                        ev += 1
                        P.mm(ps.tile, [dwa[b].tile, cxb[b].tile] + cxt[b], ps[:, 0:w], [(dwa[b][:, k, :], cxb[b][:, p0 + k - 1:p0 + k - 1 + w]) for k in range(3)])
                        P.tt(ya[b].tile, [ps.tile, Bt[b].tile], ya[b][:, t0:t0 + w], ps[:, 0:w], Bt[b][:, t0:t0 + w], ALU.mult)
                    P.dma([], [ya[b].tile], yall_d[c * 128:(c + 1) * 128, a0:T], ya[b][:, a0:T])
                P.flush()

            with ExitStack() as st:
                xpbs = [SB(st, nc, "lxpb%d" % i, [128, T + 6], BF16) for i in range(2)]
                xc = SB(st, nc, "lxc", [128, T], F32)
                xcb = SB(st, nc, "lxcb", [128, T], BF16)
                Rr = [SB(st, nc, "lR%d" % i, [128, T], F32) for i in range(2)]
                Ii = [SB(st, nc, "lI%d" % i, [128, T], F32) for i in range(2)]
                Aa = [SB(st, nc, "lA%d" % i, [128, T], F32) for i in range(2)]
                gts = [SB(st, nc, "lg%d" % i, [128, T], BF16) for i in range(2)]
                yb = SB(st, nc, "lyb", [128, T], BF16)
                wa_b = SB(st, nc, "lwa_b", [128, 2, 2, 4, 128], BF16)
                dwl = SB(st, nc, "dwl", [128, 4, 128], BF16)
                with ExitStack() as st2:
                    wa_f = SB(st2, nc, "lwa_f", [128, 2, 2, 4, 128], F32)
                    P.dma([wa_f.tile], [], wa_f[:], lwa_d[:, l])
                    P.copy(wa_b.tile, [wa_f.tile], wa_b[:], wa_f[:], eng="pool")
                    for i in range(2):
                        P.memset(xpbs[i].tile, xpbs[i][:], 0.0)
                    P.flush()
                a0 = 0 if not last else CT

                def rv(tn, lo, hi):
                    return bass.AP(tn.t, hi - 1, [[T, 128], [-1, hi - lo]])

                def l_load(c):
                    rows = slice(c * 128, (c + 1) * 128)
                    xpb = xpbs[c % 2]
                    P.dma([xpb.tile], [], xpb[:, 2:258], lrux_d[rows, 0:CT])
                    P.dma([xpb.tile], [], xpb[:, 261:261 + S], lrux_d[rows, CT:T])
                    P.dma([gts[c % 2].tile], [], gts[c % 2][:, a0:T], lrug_d[rows, a0:T])
                ev = 0
                l_load(0)
                for c in range(4):
                    xpb = xpbs[c % 2]
                    gt = gts[c % 2]
                    for k in range(4):
                        P.ts(dwl.tile, [ident.tile, lcw.tile], dwl[:, k, :], ident[:], lcw[:, l, k, c:c + 1], ALU.mult)
                    for ti, (t0, w) in enumerate(TILES):
                        p0 = t0 + (2 if ti == 0 else 5)
                        ps = PS[ev % 6]
                        ev += 1
                        P.mm(ps.tile, [dwl.tile, xpb.tile], ps[:, 0:w], [(dwl[:, k, :], xpb[:, p0 + k - 2:p0 + k - 2 + w]) for k in range(4)])
                        P.act(xc.tile, [ps.tile, lcb.tile], xc[:, t0:t0 + w], ps[:, 0:w], AF.Identity, scale=1.0, bias=lcb[:, l, c:c + 1])
                        P.copy(xcb.tile, [xc.tile], xcb[:, t0:t0 + w], xc[:, t0:t0 + w], eng="dve")
                    if c + 1 < 4:
                        l_load(c + 1)
                    for d in range(2):
                        for ti, (t0, w) in enumerate(TILES):
                            pa = PS[ev % 6]
                            pb = PS[(ev + 1) % 6]
                            ev += 2
                            P.mm(pa.tile, [wa_b.tile, xcb.tile], pa[:, 0:w], [(wa_b[:, d, 0, c, :], xcb[:, t0:t0 + w])])
                            P.mm(pb.tile, [wa_b.tile, xcb.tile], pb[:, 0:w], [(wa_b[:, d, 1, c, :], xcb[:, t0:t0 + w])])
                            P.act(Rr[d].tile, [pa.tile, lb.tile], Rr[d][:, t0:t0 + w], pa[:, 0:w], AF.Sigmoid, scale=1.0, bias=lb[:, l, 0, d, c:c + 1])
                            P.act(Ii[d].tile, [pb.tile, lb.tile], Ii[d][:, t0:t0 + w], pb[:, 0:w], AF.Sigmoid, scale=1.0, bias=lb[:, l, 1, d, c:c + 1])
                    for d in range(2):
                        P.act(Aa[d].tile, [Rr[d].tile, lsc.tile], Aa[d][:], Rr[d][:], AF.Exp, scale=lsc[:, l, 0, d, c:c + 1])
                        P.act(Rr[d].tile, [Rr[d].tile, lsc.tile], Rr[d][:], Rr[d][:], AF.Exp, scale=lsc[:, l, 1, d, c:c + 1])
                        P.tt(Ii[d].tile, [Ii[d].tile, xc.tile], Ii[d][:], Ii[d][:], xc[:], ALU.mult, eng="pool")
                    for d in range(2):
                        P.ts(Rr[d].tile, [Rr[d].tile], Rr[d][:], Rr[d][:], 1.0, ALU.min, -1.0, ALU.mult)
                    for d in range(2):
                        P.act(Rr[d].tile, [Rr[d].tile], Rr[d][:], Rr[d][:], AF.Sqrt, scale=1.0, bias=1.0)
                    P.tt(Ii[0].tile, [Ii[0].tile, Rr[0].tile], Ii[0][:], Ii[0][:], Rr[0][:], ALU.mult)
                    P.op("dve", [Rr[0].tile], [Aa[0].tile, Ii[0].tile],
                         lambda e: e.tensor_tensor_scan(out=Rr[0][:], data0=Aa[0][:], data1=Ii[0][:], initial=0.0, op0=ALU.mult, op1=ALU.add))
                    P.tt(Ii[1].tile, [Ii[1].tile, Rr[1].tile], Ii[1][:], Ii[1][:], Rr[1][:], ALU.mult)
                    P.op("dve", [Rr[1].tile], [Aa[1].tile, Ii[1].tile],
                         lambda e: e.tensor_tensor_scan(out=rv(Rr[1], 0, CT), data0=rv(Aa[1], 0, CT), data1=rv(Ii[1], 0, CT),
                                                        initial=0.0, op0=ALU.mult, op1=ALU.add))
                    P.op("dve", [Rr[1].tile], [Aa[1].tile, Ii[1].tile, Rr[1].tile],
                         lambda e: e.tensor_tensor_scan(out=rv(Rr[1], CT, T), data0=rv(Aa[1], CT, T), data1=rv(Ii[1], CT, T),
                                                        initial=Rr[1][:, 0:1], op0=ALU.mult, op1=ALU.add))
                    P.tt(Rr[0].tile, [Rr[0].tile, Rr[1].tile], Rr[0][:, a0:T], Rr[0][:, a0:T], Rr[1][:, a0:T], ALU.add)
                    P.tt(yb.tile, [Rr[0].tile, gt.tile], yb[:, a0:T], Rr[0][:, a0:T], gt[:, a0:T], ALU.mult)
                    P.dma([], [yb.tile], yall_d[512 + c * 128:512 + (c + 1) * 128, a0:T], yb[:, a0:T])
                P.flush()

            with ExitStack() as st:
                KT = SB(st, nc, "KT", [128, 8, T], BF16)
                VA = SB(st, nc, "VA", [128, 34, 8, 65], BF16)
                rope = SB(st, nc, "ropeq", [128, 2, S], F32)
                wk = SB(st, nc, "mwk", [128, 2, 512], BF16); wv = SB(st, nc, "mwv", [128, 2, 512], BF16)
                wq = SB(st, nc, "mwq", [128, 3, 768], BF16); wqp = SB(st, nc, "mwqp", [128, 3, 768], BF16)
                stkv = ExitStack()
                wstg = SB(stkv, nc, "mwst", [128, 3, 768], F32)
                kvl = [SB(stkv, nc, "kvl%d" % i, [128, 2, 512], BF16) for i in range(2)]
                ksq = SB(stkv, nc, "ksq", [128, 3, 512], BF16)
                rkv = SB(stkv, nc, "rkv", [128, 512], F32)
                rv1 = [SB(stkv, nc, "rv1%d" % i, [128, 2], F32) for i in range(2)]
                P.dma([rope.tile], [], rope[:], rope_d[:])
                P.memset(VA.tile, VA[:], 1.0)
                for (dd, nk, ncol, dst, gsb) in ((wk_d, 2, 512, wk, kvng), (wv_d, 2, 512, wv, kvng), (wq_d, 3, 768, wq, qng), (wqp_d, 3, 768, wqp, qng)):
                    P.dma([wstg.tile], [], wstg[:, 0:nk, 0:ncol], dd[l].rearrange("(kc p) n -> p kc n", p=128))
                    for kc in range(nk):
                        P.ts(dst.tile, [wstg.tile, gsb.tile], dst[:, kc, :], wstg[:, kc, 0:ncol], gsb[:, l, kc:kc + 1], ALU.mult)
                ksq2 = SB(stkv, nc, "ksq2", [128, 2, 512], BF16)
                rkv2 = SB(stkv, nc, "rkv2", [128, 512], F32)
                ksqs = [ksq, ksq2]
                rkvs = [rkv, rkv2]
                r4 = [SB(stkv, nc, "r4%d" % i, [128, 8], F32) for i in range(2)]

                KTr = Tile("KTr")

                def kv_pro(ti):
                    t0, w = TILES[ti]
                    b = ti % 2
                    kv = kvl[b]
                    P.dma([kv.tile], [], kv[:, :, 0:w], fm(kvl_d)[:, :, t0:t0 + w])
                    P.tt(ksqs[b].tile, [kv.tile], ksqs[b][:, 0:2, 0:w], kv[:, :, 0:w], kv[:, :, 0:w], ALU.mult, eng="pool")
                    P.mm(PS[7].tile, [ksqs[b].tile, ones_bf.tile], PS[7][:, 0:w], [(ones_bf[:], ksqs[b][:, kc, 0:w]) for kc in range(2)])
                    P.act(rkvs[b].tile, [PS[7].tile], rkvs[b][:, 0:w], PS[7][:, 0:w], AF.Sqrt, scale=1.0 / 256, bias=EPS)
                    P.op("dve", [rkvs[b].tile], [rkvs[b].tile], lambda e, w=w, b=b: e.reciprocal(out=rkvs[b][:, 0:w], in_=rkvs[b][:, 0:w]))
                    na = w // 128

                    def fn(e, b=b, na=na):
                        ins = None
                        for a in range(na):
                            for kc in range(2):
                                ins = e.matmul(PS[6][:, a:a + 1], ksqs[b][:, kc, a * 128:(a + 1) * 128], ones_bf[:, 0:1], start=(kc == 0), stop=(kc == 1))
                        return ins
                    P.op("pe", [PS[6].tile], [ksqs[b].tile, ones_bf.tile], fn)
                    P.act(r4[b].tile, [PS[6].tile], r4[b][:, 0:na], PS[6][:, 0:na], AF.Sqrt, scale=1.0 / 256, bias=EPS)
                    P.op("dve", [r4[b].tile], [r4[b].tile], lambda e, b=b, na=na: e.reciprocal(out=r4[b][:, 4:4 + na], in_=r4[b][:, 0:na]))

                def kv_body(ti):
                    t0, w = TILES[ti]
                    b = ti % 2
                    kv = kvl[b]
                    rk = rkvs[b]
                    for h in range(8):
                        ps = PS[h % 4]
                        P.mm(ps.tile, [wk.tile, kv.tile], ps[0:64, 0:w], [(wk[:, kc, h * 64:(h + 1) * 64], kv[:, kc, 0:w]) for kc in range(2)])
                        P.tt(KT.tile, [ps.tile, rk.tile], KT[0:64, h, t0:t0 + w], ps[0:64, 0:w], rk[0:64, 0:w], ALU.mult)
                        P.dma([KTr], [], KT[64:96, h, t0:t0 + w], kr_d[:, t0:t0 + w])
                    for a in range(w // 128):
                        ch = (t0 + a * 128) // 128
                        pv = PS[4 + a % 2]
                        P.mm(pv.tile, [kv.tile, wv.tile], pv[:, :], [(kv[:, kc, a * 128:(a + 1) * 128], wv[:, kc, :]) for kc in range(2)])
                        P.ts(VA.tile, [pv.tile, r4[b].tile], VA[:, ch, :, 0:64], pv[:, :].rearrange("p (h d) -> p h d", d=64), r4[b][:, 4 + a:5 + a], ALU.mult)
                kv_pro(0)
                for ti in range(9):
                    if ti + 1 < 9:
                        kv_pro(ti + 1)
                    kv_body(ti)
                P.flush()
                stkv.close()
                qlt = [SB(st, nc, "qlt%d" % i, [128, 3, 512], BF16) for i in range(2)]
                Qh = [SB(st, nc, "Qh%d" % i, [128, 512], BF16) for i in range(2)]
                qtmp = [SB(st, nc, "qtmp%d" % i, [128, 512], F32) for i in range(2)]
                PT = [SB(st, nc, "PT%d" % i, [128, 512], BF16) for i in range(4)]
                rd = SB(st, nc, "rd", [128, 512], F32)
                bcs = SB(st, nc, "bcs", [64, 512], F32)
                yd = [SB(st, nc, "yd%d" % i, [64, 8, 512], BF16) for i in range(2)]
                rq = [SB(st, nc, "rq%d" % i, [128, 512], F32) for i in range(2)]
                qsq = [SB(st, nc, "qsq%d" % i, [128, 3, 512], BF16) for i in range(2)]
                jobs = [(ti, h) for ti in tiles_l for h in range(8)]
                tpos = {ti: n for n, ti in enumerate(tiles_l)}

                def prologue(ti):
                    t0, w = TILES[ti]
                    b = tpos[ti] % 2
                    ql = qlt[b]
                    P.dma([ql.tile], [], ql[:, :, 0:w], fm(ql_d)[:, :, t0:t0 + w])
                    P.tt(qsq[b].tile, [ql.tile], qsq[b][:, :, 0:w], ql[:, :, 0:w], ql[:, :, 0:w], ALU.mult, eng="pool")
                    P.mm(PS[7].tile, [qsq[b].tile, ones_bf.tile], PS[7][:, 0:w], [(ones_bf[:], qsq[b][:, kc, 0:w]) for kc in range(3)])
                    P.act(rq[b].tile, [PS[7].tile], rq[b][:, 0:w], PS[7][:, 0:w], AF.Sqrt, scale=1.0 / 384, bias=EPS)
                    P.op("dve", [rq[b].tile], [rq[b].tile], lambda e, w=w, b=b: e.reciprocal(out=rq[b][:, 0:w], in_=rq[b][:, 0:w]))

                def qprep(n):
                    ti, h = jobs[n]
                    t0, w = TILES[ti]
                    b = tpos[ti] % 2
                    ql = qlt[b]
                    rr = rq[b]
                    p1, p2 = PS[5], PS[6]
                    Q = Qh[n % 2]
                    P.mm(p1.tile, [wq.tile, ql.tile], p1[0:96, 0:w], [(wq[:, kc, h * 96:(h + 1) * 96], ql[:, kc, 0:w]) for kc in range(3)])
                    P.tt(Q.tile, [p1.tile, rr.tile], Q[0:64, 0:w], p1[0:64, 0:w], rr[0:64, 0:w], ALU.mult)
                    if ti == 0:
                        P.tt(Q.tile, [p1.tile, rr.tile], Q[64:96, 0:w], p1[64:96, 0:w], rr[64:96, 0:w], ALU.mult)
                    else:
                        s0 = t0 - CT
                        P.mm(p2.tile, [wqp.tile, ql.tile], p2[0:96, 0:w], [(wqp[:, kc, h * 96:(h + 1) * 96], ql[:, kc, 0:w]) for kc in range(3)])
                        P.tt(qtmp[0].tile, [p1.tile, rope.tile], qtmp[0][64:96, 0:w], p1[64:96, 0:w], rope[64:96, 0, s0:s0 + w], ALU.mult)
                        P.tt(qtmp[1].tile, [p2.tile, rope.tile], qtmp[1][64:96, 0:w], p2[64:96, 0:w], rope[64:96, 1, s0:s0 + w], ALU.mult)
                        P.tt(qtmp[0].tile, [qtmp[0].tile, qtmp[1].tile], qtmp[0][64:96, 0:w], qtmp[0][64:96, 0:w], qtmp[1][64:96, 0:w], ALU.add, eng="pool")
                        P.tt(Q.tile, [qtmp[0].tile, rr.tile], Q[64:96, 0:w], qtmp[0][64:96, 0:w], rr[64:96, 0:w], ALU.mult, eng="pool")

                def epi1(n):
                    ti, h = jobs[n]
                    w = TILES[ti][1]
                    accp = PS[3 + n % 2]
                    P.op("dve", [rd.tile], [accp.tile], lambda e, accp=accp, w=w: e.reciprocal(out=rd[64:65, 0:w], in_=accp[64:65, 0:w]))

                def epi2(n):
                    ti, h = jobs[n]
                    t0, w = TILES[ti]
                    accp = PS[3 + n % 2]
                    ydt = yd[tpos[ti] % 2]
                    P.mm(PS[7].tile, [rd.tile, ones_f.tile], PS[7][0:64, 0:w], [(ones_f[64:65, 0:64], rd[64:65, 0:w])])
                    P.copy(bcs.tile, [PS[7].tile], bcs[:, 0:w], PS[7][0:64, 0:w], eng="dve")
                    P.tt(ydt.tile, [accp.tile, bcs.tile], ydt[:, h, 0:w], accp[0:64, 0:w], bcs[:, 0:w], ALU.mult)
                    if h == 7:
                        P.dma([], [ydt.tile], fm(yall_d[1536:2048, :], p=64)[:, :, t0:t0 + w], ydt[:, :, 0:w])

                LA = 2
                gi = 0
                prologue(jobs[0][0])
                qprep(0)
                pend = None
                for n, (ti, h) in enumerate(jobs):
                    t0, w = TILES[ti]
                    keys = list(range(0, 2)) if ti == 0 else list(range(0, 34))
                    nk = len(keys)
                    Q = Qh[n % 2]
                    accp = PS[3 + n % 2]

                    def qk(i):
                        kc = keys[i]
                        pss = PS[(gi + i) % 3]
                        P.mm1(pss.tile, [Q.tile], pss[:, 0:w], KT[0:96, h, kc * 128:(kc + 1) * 128], Q[0:96, 0:w], True, True)
                    for i in range(min(LA, nk)):
                        qk(i)
                    i_prep = min(8, nk - 1)
                    i_epi = min(14, nk - 1)
                    for i in range(nk):
                        kc = keys[i]
                        pss = PS[(gi + i) % 3]
                        pt = PT[(gi + i) % 4]
                        P.act(pt.tile, [pss.tile], pt[:, 0:w], pss[:, 0:w], AF.Exp, scale=SCALE)
                        if i + LA < nk:
                            qk(i + LA)
                        P.mm1(accp.tile, [pt.tile], accp[0:65, 0:w], VA[:, kc, h, 0:65], pt[:, 0:w], i == 0, i == nk - 1)
                        if i == i_prep and n + 1 < len(jobs):
                            if jobs[n + 1][0] != ti:
                                prologue(jobs[n + 1][0])
                            qprep(n + 1)
                        if i == i_epi and pend is not None:
                            epi2(pend)
                            pend = None
                    gi += nk
                    epi1(n)
                    pend = n
                epi2(pend)
                P.flush()

            with ExitStack() as st:
                wb = SB(st, nc, "wb", [128, 12, D], BF16)
                wbD = SB(st, nc, "wbD", [64, 8, D], BF16)
                wo = SB(st, nc, "wo", [128, 8, D], BF16)
                with ExitStack() as st2:
                    wstg = [SB(st2, nc, "ewst%d" % i, [128, 4, D], F32) for i in range(2)]
                    k = 0
                    for n in range(3):
                        b = k % 2; k += 1
                        P.dma([wstg[b].tile], [], wstg[b][:], wbr_d[l, n].rearrange("(kc p) n -> p kc n", p=128))
                        P.copy(wb.tile, [wstg[b].tile], wb[:, n * 4:(n + 1) * 4, :], wstg[b][:], eng="pool" if n % 2 else "dve")
                    for hh in range(2):
                        b = k % 2; k += 1
                        P.dma([wstg[b].tile], [], wstg[b][0:64, :, :], wbr_d[l, 3, hh * 256:(hh + 1) * 256, :].rearrange("(h p) n -> p h n", p=64))
                        P.copy(wbD.tile, [wstg[b].tile], wbD[:, hh * 4:(hh + 1) * 4, :], wstg[b][0:64, :, :], eng="pool" if hh else "dve")
                    for hh in range(2):
                        b = k % 2; k += 1
                        P.dma([wstg[b].tile], [], wstg[b][:], wout_d[l, hh * 512:(hh + 1) * 512, :].rearrange("(kc p) n -> p kc n", p=128))
                        P.copy(wo.tile, [wstg[b].tile], wo[:, hh * 4:(hh + 1) * 4, :], wstg[b][:], eng="pool" if hh else "dve")
                    P.flush()
                y3s = [SB(st, nc, "y3%d" % i, [128, 12, 512], BF16) for i in range(2)]
                yDs = [SB(st, nc, "yD%d" % i, [64, 8, 512], BF16) for i in range(2)]
                sg = [SB(st, nc, "sg%d" % i, [128, 4, 512], BF16) for i in range(3)]
                xt = [SB(st, nc, "ext%d" % i, [128, 8, 512], F32) for i in range(3)]
                mg = SB(st, nc, "mg", [128, 8, 512], BF16)
                mt = [SB(st, nc, "mt%d" % i, [128, 512], F32) for i in range(8)]
                sq = SB(st, nc, "esq", [128, 8, 512], BF16)
                rstd = SB(st, nc, "erstd", [128, 512], F32)
                tmps = [SB(st, nc, "etmp%d" % i, [128, 512], F32) for i in range(2)]
                h2s = SB(st, nc, "h2s", [128, 8, 512], BF16)
                tp4 = {ti: n for n, ti in enumerate(tiles_l)}

                def e_load(ti):
                    t0, w = TILES[ti]
                    b = tp4[ti] % 2
                    bx = tp4[ti] % 3
                    P.dma([y3s[b].tile], [], y3s[b][:, :, 0:w], fm(yall_d[0:1536, :])[:, :, t0:t0 + w])
                    P.dma([yDs[b].tile], [], yDs[b][:, :, 0:w], fm(yall_d[1536:2048, :], p=64)[:, :, t0:t0 + w])
                    P.dma([xt[bx].tile], [], xt[bx][:, :, 0:w], fm(xT_d)[:, :, t0:t0 + w])

                def e_merge(ti):
                    t0, w = TILES[ti]
                    b = tp4[ti] % 2
                    y3, yD = y3s[b], yDs[b]
                    for j in range(8):
                        sgt = sg[j % 3]
                        P.dma([sgt.tile], [], sgt[:, :, 0:w],
                              gate_d.rearrange("(n j p) t -> p n j t", n=4, p=128)[:, :, j, t0:t0 + w])
                        mo = 4 * (j % 2)
                        for n in range(4):
                            ps = PS[mo + n]
                            if n < 3:
                                pairs = [(wb[:, n * 4 + kc, j * 128:(j + 1) * 128], y3[:, n * 4 + kc, 0:w]) for kc in range(4)]
                                P.mm(ps.tile, [wb.tile, y3.tile], ps[:, 0:w], pairs)
                            else:
                                pairs = [(wbD[0:64, h, j * 128:(j + 1) * 128], yD[0:64, h, 0:w]) for h in range(8)]
                                P.mm(ps.tile, [wbD.tile, yD.tile], ps[:, 0:w], pairs)
                            P.tt(mt[mo + n].tile, [ps.tile, sgt.tile], mt[mo + n][:, 0:w], ps[:, 0:w], sgt[:, n, 0:w], ALU.mult)
                        P.tt(mt[mo].tile, [mt[mo].tile, mt[mo + 1].tile], mt[mo][:, 0:w], mt[mo][:, 0:w], mt[mo + 1][:, 0:w], ALU.add, eng="pool")
                        P.tt(mt[mo + 2].tile, [mt[mo + 2].tile, mt[mo + 3].tile], mt[mo + 2][:, 0:w], mt[mo + 2][:, 0:w], mt[mo + 3][:, 0:w], ALU.add, eng="pool")
                        P.tt(mg.tile, [mt[mo].tile, mt[mo + 2].tile], mg[:, j, 0:w], mt[mo][:, 0:w], mt[mo + 2][:, 0:w], ALU.add, eng="pool")

                def e_outproj(ti):
                    t0, w = TILES[ti]
                    b = tp4[ti] % 3
                    s = 1 if ti == 0 else 0
                    for j in range(8):
                        ps = PS[j % 4]
                        P.mm(ps.tile, [wo.tile, mg.tile], ps[:, 0:w], [(wo[:, kc, j * 128:(j + 1) * 128], mg[:, kc, 0:w]) for kc in range(8)])
                        P.stt(xt[b].tile, [ps.tile, MOD.tile, xt[b].tile], xt[b][:, j, 0:w], ps[:, 0:w], modv(l, 2, j, s), xt[b][:, j, 0:w], ALU.mult, ALU.add)
                    P.dma([], [xt[b].tile], fm(xT_d)[:, :, t0:t0 + w], xt[b][:, :, 0:w], q="act")

                def e_norm(ti):
                    t0, w = TILES[ti]
                    b = tp4[ti] % 3
                    s = 1 if ti == 0 else 0
                    norm_tile(xt[b], w, sq, rstd, tmps, PS[7], lambda c: h2s[:, c, 0:w], h2s.tile,
                              lambda c: AM[:, l, 1, c, s:s + 1], lambda c: modv(l, 3, c, s))
                    P.dma([], [h2s.tile], fm(hT_d)[:, :, t0:t0 + w], h2s[:, :, 0:w], q="act")

                e_load(tiles_l[0])
                prev = None
                for n, ti in enumerate(tiles_l):
                    if n + 1 < len(tiles_l):
                        e_load(tiles_l[n + 1])
                    e_merge(ti)
                    if prev is not None:
                        e_norm(prev)
                    e_outproj(ti)
                    prev = ti
                e_norm(prev)
                P.flush()

            tokA = CT if last else 0
            stf = ExitStack()
            w2 = SB(stf, nc, "w2", [128, 22, D], BF16)
            with ExitStack() as st:
                hT = SB(st, nc, "hT2", [128, 8, T], BF16)
                w2stg = [SB(st, nc, "w2stg%d" % i, [128, 1, D], F32) for i in range(2)]
                w2q = list(range(22))

                def load_w2(nmax):
                    for _ in range(nmax):
                        if not w2q:
                            return
                        q = w2q.pop(0)
                        b = q % 2
                        P.dma([w2stg[b].tile], [], w2stg[b][:], wf2_d[l, q * 128:(q + 1) * 128, :].rearrange("(kc p) n -> p kc n", p=128))
                        P.copy(w2.tile, [w2stg[b].tile], w2[:, q:q + 1, :], w2stg[b][:], eng="pool")
                wst = [SB(st, nc, "fwst%d" % i, [128, 8, 512], F32) for i in range(2)]
                w1b = [SB(st, nc, "w1b%d" % i, [128, 8, 512], BF16) for i in range(2)]
                w3b = [SB(st, nc, "w3b%d" % i, [128, 8, 512], BF16) for i in range(2)]
                ostg = [SB(st, nc, "fost%d" % i, [128, 4, 512], BF16) for i in range(3)]
                sa = [SB(st, nc, "fsa%d" % i, [128, 512], F32) for i in range(2)]
                hTt = [Tile("hTf%d" % i) for i in range(9)]
                for ti in tiles_l:
                    t0, w = TILES[ti]
                    P.dma([hTt[ti]], [], hT[:, :, t0:t0 + w], fm(hT_d)[:, :, t0:t0 + w])
                slabs = [(c0, min(512, DFF - c0)) for c0 in range(0, DFF, 512)]

                def load_f(i):
                    c0, n = slabs[i]
                    b = i % 2
                    P.dma([wst[0].tile], [], wst[0][:, :, 0:n], wf1_d[l, :, c0:c0 + n].rearrange("(kc p) n -> p kc n", p=128))
                    P.copy(w1b[b].tile, [wst[0].tile], w1b[b][:, :, 0:n], wst[0][:, :, 0:n], eng="pool")
                    P.dma([wst[1].tile], [], wst[1][:, :, 0:n], wf3_d[l, :, c0:c0 + n].rearrange("(kc p) n -> p kc n", p=128))
                    P.copy(w3b[b].tile, [wst[1].tile], w3b[b][:, :, 0:n], wst[1][:, :, 0:n], eng="pool")
                load_f(0)
                ev = 0
                for i, (c0, n) in enumerate(slabs):
                    if i + 1 < len(slabs):
                        load_f(i + 1)
                    load_w2(4)
                    b = i % 2
                    nch = n // 128
                    for ti in tiles_l:
                        t0, w = TILES[ti]
                        og = ostg[ev % 3]
                        for j in range(nch):
                            pa = PS[(2 * ev) % 6]
                            pb = PS[(2 * ev + 1) % 6]
                            sat = sa[ev % 2]
                            ev += 1
                            P.mm(pa.tile, [w1b[b].tile, hTt[ti]], pa[:, 0:w], [(w1b[b][:, kc, j * 128:(j + 1) * 128], hT[:, kc, t0:t0 + w]) for kc in range(8)])
                            P.mm(pb.tile, [w3b[b].tile, hTt[ti]], pb[:, 0:w], [(w3b[b][:, kc, j * 128:(j + 1) * 128], hT[:, kc, t0:t0 + w]) for kc in range(8)])
                            P.act(sat.tile, [pa.tile], sat[:, 0:w], pa[:, 0:w], AF.Silu)
                            P.tt(og.tile, [sat.tile, pb.tile], og[:, j, 0:w], sat[:, 0:w], pb[:, 0:w], ALU.mult)
                        P.dma([], [og.tile], fm(ffh_d[c0:c0 + n, :])[:, :, t0:t0 + w], og[:, 0:nch, 0:w], q="act")
                load_w2(22)
                P.flush()

            with ExitStack() as st:
                hid = [SB(st, nc, "hid%d" % i, [128, 22, 512], BF16) for i in range(2)]
                xt = [SB(st, nc, "gxt%d" % i, [128, 8, 512], F32) for i in range(2)]
                if last:
                    sq = SB(st, nc, "gsq", [128, 8, 512], BF16)
                    rstd = SB(st, nc, "grstd", [128, 512], F32)
                    tmps = [SB(st, nc, "gtmp%d" % i, [128, 512], F32) for i in range(2)]
                    ho = SB(st, nc, "gho", [128, 8, 512], F32)
                    ot = SB(st, nc, "got", [128, 4, D], F32)
                for ti in tiles_l:
                    t0, w = TILES[ti]
                    b = ti % 2
                    s = 1 if ti == 0 else 0
                    P.dma([hid[b].tile], [], hid[b][:, :, 0:w], fm(ffh_d)[:, :, t0:t0 + w])
                    P.dma([xt[b].tile], [], xt[b][:, :, 0:w], fm(xT_d)[:, :, t0:t0 + w])
                    for j in range(8):
                        ps = PS[j % 4]
                        P.mm(ps.tile, [w2.tile, hid[b].tile], ps[:, 0:w], [(w2[:, kc, j * 128:(j + 1) * 128], hid[b][:, kc, 0:w]) for kc in range(22)])
                        P.stt(xt[b].tile, [ps.tile, MOD.tile, xt[b].tile], xt[b][:, j, 0:w], ps[:, 0:w], modv(l, 5, j, s), xt[b][:, j, 0:w], ALU.mult, ALU.add)
                    if not last:
                        P.dma([], [xt[b].tile], fm(xT_d)[:, :, t0:t0 + w], xt[b][:, :, 0:w], q="act")
                    else:
                        if debug:
                            P.dma([], [xt[b].tile], fm(xT_d)[:, :, t0:t0 + w], xt[b][:, :, 0:w])
                        norm_tile(xt[b], w, sq, rstd, tmps, PS[7], lambda c: ho[:, c, 0:w], ho.tile,
                                  lambda c: fng[:, c:c + 1], lambda c: zero8[:, c:c + 1])
                        for a in range(w // 128):
                            for half in range(2):
                                ps = PS[4 + (2 * a + half) % 3]
                                items = [(ps[:, cc * 128:(cc + 1) * 128], ho[:, half * 4 + cc, a * 128:(a + 1) * 128]) for cc in range(4)]
                                P.transposes(ps.tile, [ho.tile, ident.tile], items, ident[:])
                                if half == 0:
                                    P.copy(ot.tile, [ps.tile], ot[:, a, 0:512], ps[:, :], eng="dve")
                                else:
                                    P.act(ot.tile, [ps.tile], ot[:, a, 512:1024], ps[:, :], AF.Copy)
                        r0 = t0 - CT
                        P.dma([], [ot.tile], out_d[r0:r0 + w, :].rearrange("(a p) d -> p a d", p=128), ot[:, 0:w // 128, :], q="act")
                P.flush()
            stf.close()
    return nc


_PERM32 = np.concatenate([np.arange(8, 16), np.arange(0, 8), np.arange(24, 32), np.arange(16, 24)])


def _fmaj(v, n):
    v = np.asarray(v)
    sh = v.shape[:-1]
    v = v.reshape(sh + (n, 128))
    return np.ascontiguousarray(np.moveaxis(v, -1, 0))


def prep(inp):
    f = np.float32
    sh = {}
    sh["w_mod"] = np.ascontiguousarray(inp["w_mod"], f)
    sh["bmod"] = _fmaj(inp["b_mod"], 48).astype(f)
    ng = np.stack([inp["norm1_g"], inp["norm2_g"]], axis=1)
    sh["ng"] = _fmaj(ng, 8).astype(f)
    sh["fng"] = _fmaj(inp["final_norm_g"], 8).astype(f)
    sh["w_in"] = np.ascontiguousarray(inp["w_in"], f)
    sh["w_krp"] = np.ascontiguousarray(inp["w_in"][:, :, C_KR:C_KR + 32][:, :, _PERM32], f)
    sh["convw"] = _fmaj(inp["conv_a_w"], 4).astype(f)
    sh["lcw"] = _fmaj(inp["lru_conv_w"], 4).astype(f)
    sh["lcb"] = _fmaj(inp["lru_conv_b"], 4).astype(f)
    wa = np.stack([inp["lru_w_a"], inp["lru_w_x"]], axis=2)
    bd = np.zeros((128, L, 2, 2, 4, 128), f)
    for c in range(4):
        for hh in range(2):
            blk = wa[:, :, :, 2 * c + hh]
            bd[hh * 64:(hh + 1) * 64, :, :, :, c, hh * 64:(hh + 1) * 64] = np.moveaxis(blk, 3, 0)
    sh["lwa"] = bd
    lbs = np.stack([inp["lru_b_a"], inp["lru_b_x"], inp["lru_lam"]], axis=1)
    sh["lb"] = _fmaj(lbs, 4).astype(f)
    cl = np.stack([inp["cmlp_ln_g"], inp["cmlp_ln_b"]], axis=1)
    sh["clg"] = _fmaj(cl, 4).astype(f)
    sh["cws"] = np.ascontiguousarray(np.transpose(inp["cmlp_w_s"], (3, 0, 1, 2)), f)
    sh["cbs"] = np.ascontiguousarray(np.broadcast_to(inp["cmlp_b_s"][None], (128, L, 4, 128)), f)
    sh["qng"] = _fmaj(inp["mla_q_norm_g"], 3).astype(f)
    sh["kvng"] = _fmaj(inp["mla_kv_norm_g"], 2).astype(f)
    wq = np.asarray(inp["mla_w_q_up"], f)
    sh["wq"] = np.ascontiguousarray(wq)
    wqp = wq.reshape(L, 384, 8, 96).copy()
    wqp[:, :, :, 64:96] = wqp[:, :, :, 64:96][..., _PERM32]
    sh["wqp"] = np.ascontiguousarray(wqp.reshape(L, 384, 768))
    wkv = np.asarray(inp["mla_w_kv_up"], f).reshape(L, 256, 8, 128)
    sh["wk"] = np.ascontiguousarray(wkv[..., :64].reshape(L, 256, 512))
    sh["wv"] = np.ascontiguousarray(wkv[..., 64:].reshape(L, 256, 512))
    for k in ("w_branch", "w_out", "w_ff1", "w_ff3", "w_ff2"):
        sh[k] = np.ascontiguousarray(inp[k], f)
    t = np.arange(S)
    inv = (np.float32(10000.0) ** (-np.arange(8, dtype=f) / np.float32(8))).astype(f)
    ar = (t // 64).astype(f)[:, None] * inv
    ac = (t % 64).astype(f)[:, None] * inv
    cos32 = np.concatenate([np.cos(ar), np.cos(ar), np.cos(ac), np.cos(ac)], axis=1).astype(f)
    sin32 = np.concatenate([-np.sin(ar), np.sin(ar), -np.sin(ac), np.sin(ac)], axis=1).astype(f)
    rope = np.zeros((128, 2, S), f)
    for base in (0, 64):
        rope[base:base + 32, 0] = cos32.T
        rope[base:base + 32, 1] = sin32.T
    sh["rope"] = rope
    sh["ident"] = np.eye(128, dtype=f)
    maps = []
    cc = _fmaj(inp["c_ctx"], 8).astype(f)
    for b in range(8):
        m = dict(sh)
        m["x"] = np.ascontiguousarray(inp["x"][b], f)
        m["ctx"] = np.ascontiguousarray(inp["ctx"][b], f)
        m["cvec"] = np.ascontiguousarray(np.stack([_fmaj(inp["c"][b], 8), cc], axis=-1), f)
        maps.append(m)
    return maps


def kernel(**inputs):
    inputs = {k: np.asarray(v) for k, v in inputs.items()}
    maps = prep(inputs)
    nc = build()
    res = run_bass_kernel_spmd(nc, maps, core_ids=list(range(8)))
    return np.stack([np.asarray(r["out"], np.float32) for r in res.results], axis=0)
```
